# Optimizing a Trainium2 kernel written in Bass

```python
import math
import jax, jax.numpy as jnp
from jax import lax
import numpy as np

D_MODEL = 2048
BATCH = 1
SEQ = 8192
DEPTH = 1

HEAD_DIM = 128
MIX_WIDTH = D_MODEL
FOURIER_WIDTH = MIX_WIDTH // 4
ATTN_WIDTH = MIX_WIDTH - FOURIER_WIDTH
N_Q_HEADS = ATTN_WIDTH // HEAD_DIM
GQA_GROUP = 3
N_KV_HEADS = N_Q_HEADS // GQA_GROUP
KV_WIDTH = N_KV_HEADS * HEAD_DIM
FOURIER_GROUP_DIM = 128
N_FOURIER_GROUPS = FOURIER_WIDTH // FOURIER_GROUP_DIM
IN_WIDTH = ATTN_WIDTH + 2 * KV_WIDTH + FOURIER_WIDTH
WINDOW = 128
BLOCK = 128
D_FF = ((8 * D_MODEL // 3 + 127) // 128) * 128
CONV_WIDTH = 3
EPS = 1e-6
NEG_INF = -1e30

kernel_name = "hybrid_window_gqa_fnet_convffn_block"


def _rmsnorm(x, g):
    xf = x.astype(jnp.float32)
    y = xf * lax.rsqrt(jnp.mean(xf * xf, axis=-1, keepdims=True) + EPS)
    return (y * g.astype(jnp.float32)).astype(x.dtype)


def _alibi_slopes(n_heads):
    def pow2_slopes(n):
        start = 2.0 ** (-8.0 / n)
        return [start ** (i + 1) for i in range(n)]
    if math.log2(n_heads).is_integer():
        s = pow2_slopes(n_heads)
    else:
        closest = 2 ** int(math.floor(math.log2(n_heads)))
        s = pow2_slopes(closest) + pow2_slopes(2 * closest)[0::2][: n_heads - closest]
    return jnp.asarray(np.array(s, dtype=np.float32))


def _window_attention(q, k, v, sink):
    B, S = q.shape[0], q.shape[1]
    nb = S // BLOCK
    qb = q.reshape(B, nb, BLOCK, N_KV_HEADS, GQA_GROUP, HEAD_DIM)

    def band(t):
        tb = t.reshape(B, nb, BLOCK, N_KV_HEADS, HEAD_DIM)
        tp = jnp.pad(tb, ((0, 0), (1, 1), (0, 0), (0, 0), (0, 0)))
        return jnp.concatenate([tp[:, :-2], tp[:, 1:-1], tp[:, 2:]], axis=2)

    kb, vb = band(k), band(v)
    scores = jnp.einsum('bnqkgd,bnskd->bnkgqs', qb, kb).astype(jnp.float32)
    scores = scores * (HEAD_DIM ** -0.5)

    qi = jnp.arange(BLOCK)[:, None]
    kj = jnp.arange(3 * BLOCK)[None, :]
    rel = kj - BLOCK - qi
    s_abs = jnp.arange(nb)[:, None, None] * BLOCK - BLOCK + kj[None]
    valid = (jnp.abs(rel) <= WINDOW)[None] & (s_abs >= 0) & (s_abs < S)

    slopes = _alibi_slopes(N_Q_HEADS).reshape(N_KV_HEADS, GQA_GROUP)
    alibi = -slopes[:, :, None, None] * jnp.abs(rel).astype(jnp.float32)[None, None]
    scores = scores + alibi[None, None]
    scores = jnp.where(valid[None, :, None, None], scores, NEG_INF)

    sink_b = sink.astype(jnp.float32).reshape(1, 1, N_KV_HEADS, GQA_GROUP, 1, 1)
    m = jnp.maximum(jnp.max(scores, axis=-1, keepdims=True), sink_b)
    p = jnp.exp(scores - m)
    p = p / (jnp.sum(p, axis=-1, keepdims=True) + jnp.exp(sink_b - m))
    out = jnp.einsum('bnkgqs,bnskd->bnqkgd', p.astype(v.dtype), vb)
    return out.reshape(B, S, N_Q_HEADS * HEAD_DIM)


def _fourier_mix(u, w_fourier):
    B, S = u.shape[0], u.shape[1]
    ug = u.reshape(B, S, N_FOURIER_GROUPS, FOURIER_GROUP_DIM).astype(jnp.float32)
    f = jnp.real(jnp.fft.fft2(ug, axes=(1, 3), norm='ortho'))
    y = jnp.einsum('bsgc,gcd->bsgd', f.astype(u.dtype), w_fourier)
    return y.reshape(B, S, FOURIER_WIDTH)


def _conv_ffn(h, w_up, dw_w, dw_b, w_down):
    up = h @ w_up
    gate, val = up[..., :D_FF], up[..., D_FF:]
    gate = lax.conv_general_dilated(
        gate, dw_w, window_strides=(1,), padding=((CONV_WIDTH // 2, CONV_WIDTH // 2),),
        dimension_numbers=('NWC', 'WIO', 'NWC'), feature_group_count=D_FF) + dw_b
    act = jax.nn.gelu(gate, approximate=False) * val
    return act @ w_down


def setup_inputs(seed: int = 0) -> dict:
    key = jax.random.key(seed)
    ks = jax.random.split(key, 16)
    f32 = jnp.float32
    x = jax.random.normal(ks[0], (BATCH, SEQ, D_MODEL), f32)
    norm1_g = 1.0 + 0.02 * jax.random.normal(ks[1], (D_MODEL,), f32)
    w_in = jax.random.normal(ks[2], (D_MODEL, IN_WIDTH), f32) * D_MODEL ** -0.5
    sink = 0.5 * jax.random.normal(ks[3], (N_Q_HEADS,), f32)
    w_fourier = jax.random.normal(ks[4], (N_FOURIER_GROUPS, FOURIER_GROUP_DIM, FOURIER_GROUP_DIM), f32) * FOURIER_GROUP_DIM ** -0.5
    attn_out_g = 1.0 + 0.02 * jax.random.normal(ks[5], (ATTN_WIDTH,), f32)
    fourier_out_g = 1.0 + 0.02 * jax.random.normal(ks[6], (FOURIER_WIDTH,), f32)
    w_out = jax.random.normal(ks[7], (MIX_WIDTH, D_MODEL), f32) * MIX_WIDTH ** -0.5
    norm2_g = 1.0 + 0.02 * jax.random.normal(ks[8], (D_MODEL,), f32)
    w_up = jax.random.normal(ks[9], (D_MODEL, 2 * D_FF), f32) * D_MODEL ** -0.5
    dw_w = jax.random.normal(ks[10], (CONV_WIDTH, 1, D_FF), f32) * CONV_WIDTH ** -0.5
    dw_b = 0.02 * jax.random.normal(ks[11], (D_FF,), f32)
    w_down = jax.random.normal(ks[12], (D_FF, D_MODEL), f32) * D_FF ** -0.5
    normf_g = 1.0 + 0.02 * jax.random.normal(ks[13], (D_MODEL,), f32)
    return {"x": x, "norm1_g": norm1_g, "w_in": w_in, "sink": sink, "w_fourier": w_fourier,
            "attn_out_g": attn_out_g, "fourier_out_g": fourier_out_g, "w_out": w_out,
            "norm2_g": norm2_g, "w_up": w_up, "dw_w": dw_w, "dw_b": dw_b, "w_down": w_down,
            "normf_g": normf_g}


def reference(x, norm1_g, w_in, sink, w_fourier, attn_out_g, fourier_out_g, w_out,
              norm2_g, w_up, dw_w, dw_b, w_down, normf_g):
    for _ in range(DEPTH):
        h = _rmsnorm(x, norm1_g)
        proj = h @ w_in
        q = proj[..., :ATTN_WIDTH]
        k = proj[..., ATTN_WIDTH:ATTN_WIDTH + KV_WIDTH]
        v = proj[..., ATTN_WIDTH + KV_WIDTH:ATTN_WIDTH + 2 * KV_WIDTH]
        u = proj[..., ATTN_WIDTH + 2 * KV_WIDTH:]
        a = _rmsnorm(_window_attention(q, k, v, sink), attn_out_g)
        f = _rmsnorm(_fourier_mix(u, w_fourier), fourier_out_g)
        x = x + jnp.concatenate([a, f], axis=-1) @ w_out
        x = x + _conv_ffn(_rmsnorm(x, norm2_g), w_up, dw_w, dw_b, w_down)
    return _rmsnorm(x, normf_g)
```

```python
import math
from contextlib import ExitStack
import numpy as np
import ml_dtypes
import concourse.bass as bass
import concourse.mybir as mybir
from concourse.bass_utils import run_bass_kernel_spmd

F32 = mybir.dt.float32
BF16 = mybir.dt.bfloat16
ALU = mybir.AluOpType
AF = mybir.ActivationFunctionType
AX = mybir.AxisListType

NCORES = 8
D = 2048
KC = 16
SEQ = 8192
TOK = 1024
T = 1026
TT = [(0, 342), (342, 684), (684, 1026)]
QT_ = 114
NQT = 9
NKT = 13
XH = NKT * QT_
QOFF = 2 * QT_
NH = 12
NKV = 4
DFF = 5504
NFF = 43
EPS = 1e-6
FGROUPS = [9, 9, 9, 8, 8]
ARENA = 52480

_STAGE = None
_DBG = {}


class Op:
    __slots__ = ("eng", "fn", "deps", "dma", "sig", "val", "sem", "idx")

    def __init__(self, eng, fn, deps, dma):
        self.eng, self.fn, self.deps, self.dma = eng, fn, deps, dma
        self.sig = False
        self.val = None
        self.sem = None


class Sched:
    ENGS = ("pe", "act", "dve", "pool", "sp")

    def __init__(self):
        self.ops = {e: [] for e in self.ENGS}
        self.lastw = {}
        self.readers = {}
        self.barrier = {}
        self.nops = 0

    def add(self, eng, fn, reads=(), writes=(), dma=None, extra=()):
        deps = []
        for k in reads:
            w = self.lastw.get(k)
            if w is not None:
                deps.append(w)
        for k in writes:
            w = self.lastw.get(k)
            if w is not None and (w.eng != eng or eng != "pe" or w.dma is not None or dma is not None):
                deps.append(w)
            for r in self.readers.get(k, {}).values():
                if r.eng != eng or eng != "pe" or r.dma is not None or dma is not None:
                    deps.append(r)
        deps.extend(self.barrier.values())
        deps.extend(extra)
        seen = set()
        dd = []
        for d in deps:
            if id(d) not in seen:
                seen.add(id(d))
                dd.append(d)
        op = Op(eng, fn, dd, dma)
        op.idx = self.nops
        self.nops += 1
        for d in dd:
            d.sig = True
        self.ops[eng].append(op)
        for k in reads:
            self.readers.setdefault(k, {})[(eng, dma)] = op
        for k in writes:
            self.lastw[k] = op
            self.readers[k] = {}
        return op

    def full_barrier(self):
        b = {}
        for e in self.ENGS:
            if self.ops[e]:
                b[e] = self.ops[e][-1]
        lastdma = {}
        for e in self.ENGS:
            for op in self.ops[e]:
                if op.dma is not None:
                    lastdma[op.dma] = op
        for ch, op in lastdma.items():
            b[("dma", ch)] = op
        self.barrier = b

    def emit(self, nc, block, sem_alloc):
        eng_sem = {e: sem_alloc("s_" + e) for e in self.ENGS}
        chan_sem = {}
        chan_cnt = {}
        for e in self.ENGS:
            cnt = 0
            for op in self.ops[e]:
                if op.dma is not None:
                    if op.dma not in chan_sem:
                        chan_sem[op.dma] = sem_alloc("d_" + op.dma)
                        chan_cnt[op.dma] = 0
                    chan_cnt[op.dma] += 16
                    op.val = chan_cnt[op.dma]
                    op.sem = chan_sem[op.dma]
                elif op.sig:
                    cnt += 1
                    op.val = cnt
                    op.sem = eng_sem[e]

        def runner(ename):
            def run(e):
                waited = {}
                for op in self.ops[ename]:
                    for d in op.deps:
                        key = id(d.sem)
                        if waited.get(key, 0) < d.val:
                            e.wait_ge(d.sem, d.val)
                            waited[key] = d.val
                    ins = op.fn(e)
                    if ins is None:
                        if not op.sig:
                            continue
                        ins = e.nop()
                    if op.dma is not None:
                        ins.then_inc(op.sem, 16)
                    elif op.sig:
                        ins.then_inc(op.sem, 1)
            return run

        block.tensor(runner("pe"))
        block.scalar(runner("act"))
        block.vector(runner("dve"))
        block.gpsimd(runner("pool"))
        block.sync(runner("sp"))


FSPLIT = 18


class LimitSched:
    def __init__(self, real, n):
        self.real, self.n = real, n

    def add(self, *a, **k):
        if self.n <= 0:
            return None
        self.n -= 1
        return self.real.add(*a, **k)

    def full_barrier(self):
        self.real.full_barrier()


class NullSched:
    def add(self, *a, **k):
        return None

    def full_barrier(self):
        pass


def build_program(part):
    nc = bass.Bass("TRN2", target_bir_lowering=False)

    def din(name, shape, dt=F32, parts=(1, 2, 3)):
        if part != 0 and part not in parts:
            return None
        return nc.dram_tensor(name, list(shape), dt, kind="ExternalInput").ap()

    def dout(name, shape, dt, parts):
        if part not in parts and not (part == 0 and name == "yT"):
            return None
        return nc.dram_tensor(name, list(shape), dt, kind="ExternalOutput").ap()

    NB = _DBG.get("nb", 16)
    NQ = _DBG.get("nq", 16)
    xT = din("xT", [D, 512 * NB], parts=(1,))
    xh = din("xh", [D, XH], parts=(2,))
    w_u = din("w_u", [D, 512], parts=(1,))
    w_qkv = din("w_qkv", [D, 2560], parts=(2,))
    w_out = din("w_out", [D, D], parts=(2,))
    nf_part = {0: NFF, 2: FSPLIT, 3: NFF - FSPLIT}.get(part, 1)
    w_upg = din("w_upg", [D, nf_part * 128], parts=(2, 3))
    w_upv = din("w_upv", [D, nf_part * 128], parts=(2, 3))
    w_dn = din("w_dn", [nf_part * 128, D], parts=(2, 3))
    w_fou = din("w_fou", [4, 128, 128], parts=(1,))
    tab = din("tab", [3, 512 * NQ, 2 * 342], BF16, parts=(1,))
    cc128 = din("cc128", [128, 2 * 128], BF16, parts=(1,))
    abias = din("abias", [QT_, NH * 5 * QT_], parts=(2,))
    par = din("par", [128, 320])
    cst = din("cst", [128, 256])
    fT_o = dout("fT", [128, 4 * T], BF16, (1,))
    fT_i = din("fT", [128, 4 * T], BF16, parts=(2,)) if part != 0 else None
    x1_o = dout("x1o", [128, KC * T], F32, (2,))
    h2_o = dout("h2o", [128, KC * T], BF16, (2,))
    x1_i = din("x1i", [128, KC * T], F32, parts=(3,)) if part != 0 else None
    h2_i = din("h2i", [128, KC * T], BF16, parts=(3,)) if part != 0 else None
    yT = dout("yT", [D, TOK], F32, (3,))
    dbg = None
    if _STAGE is not None:
        dbg = nc.dram_tensor("dbg", [128, _STAGE["n"]], F32, kind="ExternalOutput").ap()

    S_real = Sched()
    S_null = NullSched()
    S = S_real

    with ExitStack() as stack:
        arena = stack.enter_context(nc.sbuf_tensor("arena", [128, ARENA], F32))
        ps = stack.enter_context(nc.psum_tensor("ps", [128, 8, 512], F32))
        sem_pool = [stack.enter_context(nc.semaphore("sm%d" % i)) for i in range(56)]
        block = stack.enter_context(nc.Block())
        def f32v(off, n):
            return arena[:, off:off + n]

        def bfv(off, n_bf):
            assert n_bf % 2 == 0
            return arena[:, off:off + n_bf // 2].bitcast(BF16)

        cur = [0]

        def alloc(nwords):
            o = cur[0]
            cur[0] += nwords
            assert cur[0] <= ARENA, (cur[0], ARENA)
            return o

        o_par = alloc(320)
        P_ = f32v(o_par, 320)
        g1T = P_[:, 0:16]
        g2T = P_[:, 16:32]
        gnT = P_[:, 32:48]
        gaT = P_[:, 48:60]
        gfoT = P_[:, 60:64]
        sinkB = P_[:, 64:76]
        kvalid = P_[:, 76:89]
        qv = P_[:, 89:91]
        dwT = P_[:, 96:96 + 43 * 4].rearrange("p (f k) -> p f k", k=4)
        o_es = alloc(12)
        es = f32v(o_es, 12)
        o_eps = alloc(2)
        epsT = f32v(o_eps, 1)
        o_id = alloc(128)
        identf = f32v(o_id, 128)
        o_idb = alloc(64)
        identb = bfv(o_idb, 128)
        o_on = alloc(64)
        onesb = bfv(o_on, 128)
        o_vo = alloc(NKT * 64)
        vones = bfv(o_vo, NKT * 128).rearrange("p (j m) -> p j m", j=NKT)
        o_mix = alloc(KC * T // 2)
        mixT = bfv(o_mix, KC * T).rearrange("p (c t) -> p c t", c=KC)
        base = cur[0]

        ld_par = S.add("sp", lambda e: e.dma_start(out=P_, in_=par), writes=["par"], dma="par")
        ld_id = S.add("sp", lambda e: e.dma_start(out=identf, in_=cst[:, 0:128]), writes=["identf"], dma="cst")
        S.add("dve", lambda e: e.tensor_copy(out=identb, in_=identf), reads=["identf"], writes=["identb"])
        S.add("dve", lambda e: e.memset(onesb, 1.0), writes=["onesb"])
        S.add("dve", lambda e: e.memset(epsT, EPS), writes=["epsT"])
        S.add("act", lambda e: e.activation(out=es, in_=sinkB, func=AF.Exp), reads=["par"], writes=["es"])
        for j in range(NKT):
            S.add("dve", lambda e, j=j: e.tensor_scalar(out=vones[:, j, :], in0=onesb, scalar1=kvalid[:, j:j + 1],
                                                         scalar2=None, op0=ALU.mult),
                  reads=["par", "onesb"], writes=[("vones", j)])

        def rstd_from(out_ap, in_ap, n, rkeys, wkeys):
            S.add("act", lambda e: e.activation(out=out_ap, in_=in_ap, func=AF.Ln, scale=1.0 / n, bias=epsT),
                  reads=list(rkeys) + ["epsT"], writes=wkeys)
            S.add("act", lambda e: e.activation(out=out_ap, in_=out_ap, func=AF.Exp, scale=-0.5), reads=wkeys, writes=wkeys)

        def dump(ap_f32_view, n):
            S.full_barrier()
            S.add("sp", lambda e: e.dma_start(out=dbg[:, 0:n], in_=ap_f32_view), dma="dbg")
            S.full_barrier()
            S.add("sp", lambda e: None)

        o_E = ARENA - (NH * 570 // 2 + 2 * 570)
        Eb = bfv(o_E, NH * 570).rearrange("p (h r q) -> p h r q", h=NH, r=5)
        abs_ = [f32v(o_E + NH * 570 // 2 + k * 570, 570) for k in range(2)]
        S = S_real if part in (0, 1) else S_null
        cur[0] = base
        o_U = alloc(64 * 512 // 2)
        U = bfv(o_U, 64 * 512).rearrange("p (i n) -> p i n", i=64)
        f_base = cur[0]
        o_wus = alloc(KC * 512)
        wus = f32v(o_wus, KC * 512).rearrange("p (c n) -> p c n", c=KC)
        o_wub = alloc(KC * 512 // 2)
        wub = bfv(o_wub, KC * 512).rearrange("p (c n) -> p c n", c=KC)
        o_xb = [alloc(KC * 512 // 2) for _ in range(2)]
        xb = [bfv(o, KC * 512).rearrange("p (c n) -> p c n", c=KC) for o in o_xb]
        o_sm = alloc(64 * 2)
        ssu = f32v(o_sm, 64)
        rsu = f32v(o_sm + 64, 64)
        o_dg = [alloc(128) for _ in range(2)]
        dgt = [f32v(o, 128) for o in o_dg]

        for q4 in range(4):
            S.add("sp", lambda e, q4=q4: e.dma_start(
                out=wus[:, 4 * q4:4 * q4 + 4, :],
                in_=w_u[512 * q4:512 * (q4 + 1), :].rearrange("(c p) n -> p c n", p=128)),
                writes=[("wus", q4)], dma="wus%d" % q4)
        for c in range(KC):
            fold_last = S.add("dve", lambda e, c=c: e.tensor_scalar(out=wub[:, c, :], in0=wus[:, c, :], scalar1=g1T[:, c:c + 1],
                                                                     scalar2=None, op0=ALU.mult),
                              reads=[("wus", c // 4), "par"], writes=[("wub", c)])

        xstg = [wus[:, :, 0:256], wus[:, :, 256:512]]

        def load_xb(b):
            sl = b % 2
            for hf in range(2):
                S.add("sp", lambda e, hf=hf: e.dma_start(
                    out=xstg[hf], in_=xT[:, b * 512 + hf * 256:b * 512 + (hf + 1) * 256].rearrange("(c p) n -> p c n", p=128)),
                    reads=[], writes=[("xstg", hf)], dma="xs%d" % hf,
                    extra=([fold_last] if (b == 0 and fold_last is not None) else []))
                if hf == 0:
                    S.add("dve", lambda e, hf=hf: e.tensor_copy(out=xb[sl][:, :, 0:256], in_=xstg[0]),
                          reads=[("xstg", 0)], writes=[("xb", sl)])
                else:
                    S.add("act", lambda e, hf=hf: e.activation(out=xb[sl][:, :, 256:512], in_=xstg[1], func=AF.Copy),
                          reads=[("xstg", 1)], writes=[("xbh", sl)])

        if "F1" in _DBG.get("skip", ()):
            S = S_null
        load_xb(0)
        def e_build(h):
            sl = h % 2
            S_real.add("pool", lambda e: e.dma_start(out=abs_[sl][0:QT_, :], in_=abias[:, h * 570:(h + 1) * 570]),
                       writes=[("abs", sl)], dma="abs%d" % sl)
            S_real.add("act", lambda e: e.activation(
                out=Eb[0:QT_, h, :, :], in_=abs_[sl][0:QT_, :].rearrange("p (r q) -> p r q", r=5), func=AF.Exp),
                reads=[("abs", sl)], writes=[("E", h)])

        e_done = set()
        for b in range(NB):
            if b + 1 < NB:
                load_xb(b + 1)
            if part == 0 and 2 <= b < 2 + NH:
                e_build(b - 2)
                e_done.add(b - 2)
            sl = b % 2
            for i4 in range(4):
                i = b * 4 + i4
                pb = i % 2
                cols = slice(i4 * 128, (i4 + 1) * 128)
                for c in range(KC):
                    S.add("pe", lambda e, c=c, sl=sl, cols=cols, pb=pb: e.matmul(
                        ps[:, pb, 0:512], lhsT=xb[sl][:, c, cols], rhs=wub[:, c, :], start=(c == 0), stop=(c == KC - 1)),
                        reads=[(("xb", sl) if i4 < 2 else ("xbh", sl)), ("wub", c)], writes=[("ps", pb)])
                for c in range(KC):
                    S.add("pe", lambda e, c=c, sl=sl, cols=cols, pb=pb: e.matmul(
                        ps[:, 2 + pb, 0:128], lhsT=xb[sl][:, c, cols], rhs=xb[sl][:, c, cols],
                        start=(c == 0), stop=(c == KC - 1)),
                        reads=[(("xb", sl) if i4 < 2 else ("xbh", sl))], writes=[("ps", 2 + pb)])
                S.add("dve", lambda e, pb=pb: e.tensor_tensor(out=dgt[pb], in0=ps[:, 2 + pb, 0:128], in1=identf, op=ALU.mult),
                      reads=[("ps", 2 + pb), "identf"], writes=[("dgt", pb)])
                S.add("dve", lambda e, pb=pb, i=i: e.reduce_sum(out=ssu[:, i:i + 1], in_=dgt[pb], axis=AX.X),
                      reads=[("dgt", pb)], writes=[("ssu", i)])
                rstd_from(rsu[:, i:i + 1], ssu[:, i:i + 1], float(D), [("ssu", i)], [("rsu", i)])
                S.add("act", lambda e, pb=pb, i=i: e.activation(out=U[:, i, :], in_=ps[:, pb, 0:512], func=AF.Copy,
                                                                 scale=rsu[:, i:i + 1]),
                      reads=[("ps", pb), ("rsu", i)], writes=[("U", i)])

        if _STAGE and _STAGE["name"] == "U":
            dump(arena[:, o_U:o_U + _STAGE["n"]], _STAGE["n"])

        S = S_real if part in (0, 1) else S_null
        if "F2" in _DBG.get("skip", ()):
            S = S_null
        S.full_barrier()
        cur[0] = f_base
        o_tab = [alloc(4 * 684 // 2) for _ in range(3)]
        tabs = [bfv(o, 4 * 684).rearrange("p (k n) -> p k n", k=4) for o in o_tab]
        o_AB = alloc(4 * 2 * T // 2)
        ABT = bfv(o_AB, 4 * 2 * T).rearrange("p (g a t) -> p g a t", g=4, a=2)
        o_ys = alloc(4 * T)
        ysb = f32v(o_ys, 4 * T).rearrange("p (g t) -> p g t", g=4)
        o_sqf = alloc(4 * T // 2)
        sqf = bfv(o_sqf, 4 * T).rearrange("p (g t) -> p g t", g=4)
        o_rf = alloc(T)
        rstdf = f32v(o_rf, T)
        o_mc = alloc(4 * 2 * 128 // 2)
        Mc = bfv(o_mc, 4 * 2 * 128).rearrange("p (g a n) -> p g a n", g=4, a=2)
        o_wf = alloc(4 * 128 // 2)
        wfb = bfv(o_wf, 4 * 128).rearrange("p (g n) -> p g n", g=4)
        o_cc = alloc(2 * 128 // 2)
        ccb = bfv(o_cc, 2 * 128).rearrange("p (a n) -> p a n", a=2)


        ntab = 0

        def load_tab(p, q):
            nonlocal ntab
            sl = ntab % 3
            ntab += 1
            S.add("sp", lambda e: e.dma_start(
                out=tabs[sl], in_=tab[p, q * 512:(q + 1) * 512, :].rearrange("(k s) n -> s k n", s=128)),
                writes=[("tab", sl)], dma="tab%d" % sl)
            return sl

        seq = [(p, q) for p in range(3) for q in range(NQ)]
        slots = {}
        slots[0] = load_tab(*seq[0])
        slots[1] = load_tab(*seq[1])
        for n_, (p, q) in enumerate(seq):
            if n_ + 2 < len(seq):
                slots[n_ + 2] = load_tab(*seq[n_ + 2])
            sl = slots[n_]
            lo, hi = TT[p]
            for k in range(4):
                i = q * 4 + k
                for g in range(4):
                    for a in range(2):
                        bk = g * 2 + a
                        S.add("pe", lambda e, i=i, g=g, a=a, bk=bk, sl=sl, k=k: e.matmul(
                            ps[:, bk, 0:342], lhsT=U[:, i, g * 128:(g + 1) * 128], rhs=tabs[sl][:, k, a * 342:(a + 1) * 342],
                            start=(i == 0), stop=(i == 4 * NQ - 1)),
                            reads=[("U", i), ("tab", sl)], writes=[("ps", bk)])
            if q == NQ - 1:
                for g in range(4):
                    for a in range(2):
                        bk = g * 2 + a
                        eng = "act" if a == 0 else "dve"
                        if eng == "act":
                            S.add("act", lambda e, g=g, a=a, bk=bk, lo=lo, hi=hi: e.activation(
                                out=ABT[:, g, a, lo:hi], in_=ps[:, bk, 0:342], func=AF.Copy),
                                reads=[("ps", bk)], writes=[("ABT", g, a, p)])
                        else:
                            S.add("dve", lambda e, g=g, a=a, bk=bk, lo=lo, hi=hi: e.tensor_copy(
                                out=ABT[:, g, a, lo:hi], in_=ps[:, bk, 0:342]),
                                reads=[("ps", bk)], writes=[("ABT", g, a, p)])

        S = S_real if part in (0, 1) else S_null
        if "F3" in _DBG.get("skip", ()):
            S = S_null
        if "f3n" in _DBG and part == 1:
            S = LimitSched(S_real, _DBG["f3n"])
        S.add("pool", lambda e: e.dma_start(out=wfb, in_=w_fou.rearrange("g k n -> k g n")), writes=["wfb"], dma="wfb")
        S.add("sp", lambda e: e.dma_start(out=ccb, in_=cc128.rearrange("p (a n) -> p a n", a=2)), writes=["ccb"], dma="ccb")
        for g in range(4):
            for a in range(2):
                S.add("pe", lambda e, g=g, a=a: e.matmul(ps[:, a, 0:128], lhsT=ccb[:, a, :], rhs=wfb[:, g, :],
                                                         start=True, stop=True),
                      reads=["ccb", "wfb"], writes=[("ps", a)])
                S.add("act", lambda e, g=g, a=a: e.activation(out=Mc[:, g, a, :], in_=ps[:, a, 0:128], func=AF.Copy,
                                                              scale=(1.0 if a == 0 else -1.0)),
                      reads=[("ps", a)], writes=[("Mc", g, a)])
        for p, (lo, hi) in enumerate(TT):
            for g in range(4):
                bk = 2 + (g % 2)
                for a in range(2):
                    S.add("pe", lambda e, g=g, a=a, bk=bk, lo=lo, hi=hi: e.matmul(
                        ps[:, bk, 0:342], lhsT=Mc[:, g, a, :], rhs=ABT[:, g, a, lo:hi], start=(a == 0), stop=(a == 1)),
                        reads=[("Mc", g, a), ("ABT", g, a, p)], writes=[("ps", bk)])
                S.add("dve", lambda e, g=g, bk=bk, lo=lo, hi=hi: e.tensor_copy(out=ysb[:, g, lo:hi], in_=ps[:, bk, 0:342]),
                      reads=[("ps", bk)], writes=[("ysb", g, p)])
                S.add("dve", lambda e, g=g, lo=lo, hi=hi: e.tensor_tensor(out=sqf[:, g, lo:hi], in0=ysb[:, g, lo:hi],
                                                                           in1=ysb[:, g, lo:hi], op=ALU.mult),
                      reads=[("ysb", g, p)], writes=[("sqf", g, p)])
            for g in range(4):
                S.add("pe", lambda e, g=g, lo=lo, hi=hi, p=p: e.matmul(ps[:, 4 + p, 0:342], lhsT=onesb, rhs=sqf[:, g, lo:hi],
                                                                       start=(g == 0), stop=(g == 3)),
                      reads=[("sqf", g, p), "onesb"], writes=[("ps", 4 + p)])
            rstd_from(rstdf[:, lo:hi], ps[:, 4 + p, 0:342], 512.0, [("ps", 4 + p)], [("rstdf", p)])
            for g in range(4):
                S.add("dve", lambda e, g=g, lo=lo, hi=hi: e.scalar_tensor_tensor(
                    out=mixT[:, 12 + g, lo:hi], in0=ysb[:, g, lo:hi], scalar=gfoT[:, g:g + 1], in1=rstdf[:, lo:hi],
                    op0=ALU.mult, op1=ALU.mult),
                    reads=[("ysb", g, p), ("rstdf", p), "par"], writes=[("mix", 12 + g, p)])

        if _STAGE and _STAGE["name"] == "F":
            dump(arena[:, o_mix:o_mix + _STAGE["n"]], _STAGE["n"])

        mixf = arena[:, o_mix + 12 * (T // 2):o_mix + 16 * (T // 2)].bitcast(BF16)
        if part == 1:
            S = S_real
            S.full_barrier()
            S.add("sp", lambda e: e.dma_start(out=fT_o, in_=mixf), dma="fTo")
            S.full_barrier()
            S.add("sp", lambda e: None)
        S = S_real if part in (0, 2) else S_null
        if part == 2:
            S.add("sp", lambda e: e.dma_start(out=mixf, in_=fT_i), writes=[("mix", 12 + g, p) for g in range(4) for p in range(3)],
                  dma="fTi")
        S.full_barrier()
        cur[0] = base
        o_QT = alloc(NH * T // 2)
        QTs = bfv(o_QT, NH * T).rearrange("p (h t) -> p h t", h=NH)
        o_KT = alloc(NKV * XH // 2)
        KTs = bfv(o_KT, NKV * XH).rearrange("p (h t) -> p h t", h=NKV)
        o_V = alloc(NKT * 512 // 2)
        Vs = bfv(o_V, NKT * 512).rearrange("p (j n) -> p j n", j=NKT)
        att_base = cur[0]
        o_xg = alloc(KC * XH // 2)
        xg = bfv(o_xg, KC * XH).rearrange("p (c t) -> p c t", c=KC)
        o_r1 = alloc(XH)
        rstd1 = f32v(o_r1, XH)
        o_VT = alloc(NKV * XH // 2)
        VTs = bfv(o_VT, NKV * XH).rearrange("p (h t) -> p h t", h=NKV)
        o_xhs = [alloc(XH) for _ in range(2)]
        xhs = [f32v(o, XH) for o in o_xhs]
        o_sqn = [alloc(XH // 2) for _ in range(2)]
        sqn = [bfv(o, XH) for o in o_sqn]
        o_wq = [alloc(KC * 256 // 2) for _ in range(2)]
        wq = [bfv(o, KC * 256).rearrange("p (c n) -> p c n", c=KC) for o in o_wq]
        XT3 = [(0, 494), (494, 988), (988, 1482)]

        def load_wq(n_):
            sl = n_ % 2
            S.add("pool", lambda e: e.dma_start(
                out=wq[sl], in_=w_qkv[:, n_ * 256:(n_ + 1) * 256].rearrange("(c p) n -> p c n", p=128)),
                writes=[("wq", sl)], dma="wq%d" % sl)

        load_wq(0)
        load_wq(1)
        for c in range(KC):
            sl = c % 2
            S.add("sp", lambda e, c=c, sl=sl: e.dma_start(out=xhs[sl], in_=xh[c * 128:(c + 1) * 128, :]),
                  writes=[("xhs", sl)], dma="xhs%d" % sl)
            S.add("act", lambda e, c=c, sl=sl: e.activation(out=xg[:, c, :], in_=xhs[sl], func=AF.Copy,
                                                             scale=g1T[:, c:c + 1]),
                  reads=[("xhs", sl), "par"], writes=[("xg", c)])
            S.add("dve", lambda e, c=c, sl=sl: e.tensor_tensor(out=sqn[sl], in0=xhs[sl], in1=xhs[sl], op=ALU.mult),
                  reads=[("xhs", sl)], writes=[("sqn", sl)])
            for n_, (lo, hi) in enumerate(XT3):
                S.add("pe", lambda e, c=c, sl=sl, n_=n_, lo=lo, hi=hi: e.matmul(
                    ps[:, n_, 0:494], lhsT=onesb, rhs=sqn[sl][:, lo:hi], start=(c == 0), stop=(c == KC - 1)),
                    reads=[("sqn", sl), "onesb"], writes=[("ps", n_)])
        for n_, (lo, hi) in enumerate(XT3):
            rstd_from(rstd1[:, lo:hi], ps[:, n_, 0:494], float(D), [("ps", n_)], [("rstd1", n_)])

        for cc in range(20):
            pair = cc // 2
            sl = pair % 2
            off = (cc % 2) * 128
            if cc < 12:
                tiles = [(QOFF + lo, QOFF + hi) for (lo, hi) in TT]
            else:
                tiles = XT3
            bb = 3 * (cc % 2)
            for n_, (lo, hi) in enumerate(tiles):
                w = hi - lo
                for c in range(KC):
                    S.add("pe", lambda e, c=c, sl=sl, off=off, bb=bb, n_=n_, lo=lo, hi=hi, w=w: e.matmul(
                        ps[:, bb + n_, 0:w], lhsT=wq[sl][:, c, off:off + 128], rhs=xg[:, c, lo:hi],
                        start=(c == 0), stop=(c == KC - 1)),
                        reads=[("wq", sl), ("xg", c)], writes=[("ps", bb + n_)])
            for n_, (lo, hi) in enumerate(tiles):
                w = hi - lo
                if cc < 12:
                    dst = QTs[:, cc, lo - QOFF:hi - QOFF]
                    wk = [("QT", cc, n_)]
                elif cc < 16:
                    dst = KTs[:, cc - 12, lo:hi]
                    wk = [("KT", cc - 12, n_)]
                else:
                    dst = VTs[:, cc - 16, lo:hi]
                    wk = [("VT", cc - 16, n_)]
                rk = [("ps", bb + n_), ("rstd1", 0), ("rstd1", 1), ("rstd1", 2)]
                S.add("dve", lambda e, dst=dst, bb=bb, n_=n_, lo=lo, hi=hi, w=w: e.tensor_tensor(
                    out=dst, in0=ps[:, bb + n_, 0:w], in1=rstd1[:, lo:hi], op=ALU.mult), reads=rk, writes=wk)
            if cc % 2 == 1 and pair + 2 < 10:
                load_wq(pair + 2)

        psb = [ps[:, 6, :].bitcast(BF16), ps[:, 7, :].bitcast(BF16)]
        for j in range(NKT):
            pb = j % 2
            for h in range(NKV):
                S.add("pe", lambda e, j=j, h=h, pb=pb: e.transpose(
                    psb[pb][0:QT_, h * 128:(h + 1) * 128], VTs[:, h, j * QT_:(j + 1) * QT_], identb),
                    reads=[("VT", h, 0), ("VT", h, 1), ("VT", h, 2), "identb"], writes=[("ps", 6 + pb)])
            S.add("act", lambda e, j=j, pb=pb: e.activation(out=Vs[0:QT_, j, :], in_=psb[pb][0:QT_, 0:512], func=AF.Copy),
                  reads=[("ps", 6 + pb)], writes=[("V", j)])

        if _STAGE and _STAGE["name"] == "QKV":
            dump(arena[:, o_QT:o_QT + _STAGE["n"]], _STAGE["n"])

        S.full_barrier()
        cur[0] = att_base
        o_att = alloc(NH * T)
        attT = f32v(o_att, NH * T).rearrange("p (h t) -> p h t", h=NH)
        o_ex = [alloc(171) for _ in range(4)]
        exs = [bfv(o, 342) for o in o_ex]
        o_pt = [alloc(171) for _ in range(4)]
        pts = [bfv(o, 342) for o in o_pt]
        assert cur[0] <= o_E
        o_dn = [alloc(342) for _ in range(2)]
        dns = [f32v(o, 342) for o in o_dn]
        o_sqa = [alloc(T // 2) for _ in range(2)]
        sqa = [bfv(o, T) for o in o_sqa]
        o_ra = alloc(T)
        rstda = f32v(o_ra, T)

        if part in (0, 2):
            for h in range(NH):
                if h not in e_done:
                    e_build(h)
        sc = 128.0 ** -0.5
        LOOK = 3
        steps = [(i, kvh, r) for i in range(NQT) for kvh in range(NKV) for r in range(5)]

        def emit_S(n):
            i, kvh, r = steps[n]
            j = i + r
            sb_ = n % 4
            kc = slice(j * QT_, (j + 1) * QT_)
            qc = slice(i * QT_, (i + 1) * QT_)
            S.add("pe", lambda e: e.matmul(
                ps[0:QT_, sb_, 0:342].rearrange("p (h q) -> p h q", h=3),
                lhsT=KTs[:, kvh, kc], rhs=QTs[:, 3 * kvh:3 * kvh + 3, qc], start=True, stop=True),
                reads=[("KT", kvh, 0), ("KT", kvh, 1), ("KT", kvh, 2)] +
                      [("QT", 3 * kvh + hh, n_) for hh in range(3) for n_ in range(3)],
                writes=[("ps", sb_)])

        def emit_norm(i, kvh, ucnt, qc, ob, db):
            dsl = ucnt % 2
            for hh in range(3):
                h = 3 * kvh + hh
                S.add("act", lambda e, hh=hh, h=h, db=db, dsl=dsl: e.activation(
                    out=dns[dsl][:, hh * QT_:(hh + 1) * QT_], in_=ps[:, db, hh * QT_:(hh + 1) * QT_],
                    func=AF.Ln, bias=es[:, h:h + 1]),
                    reads=[("ps", db), "es"], writes=[("dn", dsl, hh)])
            S.add("act", lambda e, dsl=dsl: e.activation(out=dns[dsl], in_=dns[dsl], func=AF.Exp, scale=-1.0),
                  reads=[("dn", dsl, 0), ("dn", dsl, 1), ("dn", dsl, 2)], writes=[("rc", dsl)])
            S.add("dve", lambda e, kvh=kvh, qc=qc, ob=ob, dsl=dsl: e.tensor_tensor(
                out=attT[:, 3 * kvh:3 * kvh + 3, qc],
                in0=ps[:, ob, 0:342].rearrange("p (h q) -> p h q", h=3),
                in1=dns[dsl].rearrange("p (h q) -> p h q", h=3), op=ALU.mult),
                reads=[("ps", ob), ("rc", dsl)], writes=[("att", 3 * kvh + hh, i) for hh in range(3)])

        pending_norm = []
        for n in range(min(LOOK, len(steps))):
            emit_S(n)
        for n, (i, kvh, r) in enumerate(steps):
            if n + LOOK < len(steps):
                emit_S(n + LOOK)
            ucnt = n // 5
            qc = slice(i * QT_, (i + 1) * QT_)
            ob = 4 + (ucnt % 2)
            db = 6 + (ucnt % 2)
            j = i + r
            sb_ = n % 4
            es_ = n % 4
            S.add("act", lambda e, sb_=sb_, es_=es_: e.activation(
                out=exs[es_][0:QT_, :], in_=ps[0:QT_, sb_, 0:342], func=AF.Exp, scale=sc),
                reads=[("ps", sb_)], writes=[("ex", es_)])
            S.add("dve", lambda e, es_=es_, kvh=kvh, r=r: e.tensor_tensor(
                out=pts[es_][0:QT_, :].rearrange("p (h q) -> p h q", h=3),
                in0=exs[es_][0:QT_, :].rearrange("p (h q) -> p h q", h=3),
                in1=Eb[0:QT_, 3 * kvh:3 * kvh + 3, r, :], op=ALU.mult),
                reads=[("ex", es_)] + [("E", 3 * kvh + hh) for hh in range(3)], writes=[("pt", es_)])
            S.add("pe", lambda e, j=j, kvh=kvh, es_=es_, ob=ob, r=r: e.matmul(
                ps[:, ob, 0:342], lhsT=Vs[0:QT_, j, kvh * 128:(kvh + 1) * 128], rhs=pts[es_][0:QT_, :],
                start=(r == 0), stop=(r == 4)),
                reads=[("V", j), ("pt", es_)], writes=[("ps", ob)])
            S.add("pe", lambda e, j=j, es_=es_, db=db, r=r: e.matmul(
                ps[:, db, 0:342], lhsT=vones[0:QT_, j, :], rhs=pts[es_][0:QT_, :],
                start=(r == 0), stop=(r == 4)),
                reads=[("vones", j), ("pt", es_)], writes=[("ps", db)])
            for rel_n, fn_ in list(pending_norm):
                if rel_n <= n:
                    fn_()
                    pending_norm.remove((rel_n, fn_))
            if r == 4:
                pending_norm.append((n + 2, (lambda i=i, kvh=kvh, ucnt=ucnt, qc=qc, ob=ob, db=db: emit_norm(i, kvh, ucnt, qc, ob, db))))
        for rel_n, fn_ in pending_norm:
            fn_()

        for h in range(NH):
            sl = h % 2
            S.add("dve", lambda e, h=h, sl=sl: e.tensor_tensor(out=sqa[sl], in0=attT[:, h, :], in1=attT[:, h, :], op=ALU.mult),
                  reads=[("att", h, i) for i in range(NQT)], writes=[("sqa", sl)])
            for n_, (lo, hi) in enumerate(TT):
                S.add("pe", lambda e, h=h, sl=sl, n_=n_, lo=lo, hi=hi: e.matmul(
                    ps[:, n_, 0:342], lhsT=onesb, rhs=sqa[sl][:, lo:hi], start=(h == 0), stop=(h == NH - 1)),
                    reads=[("sqa", sl), "onesb"], writes=[("ps", n_)])
        for n_, (lo, hi) in enumerate(TT):
            rstd_from(rstda[:, lo:hi], ps[:, n_, 0:342], 1536.0, [("ps", n_)], [("rstda", n_)])
        for h in range(NH):
            S.add("dve", lambda e, h=h: e.scalar_tensor_tensor(
                out=mixT[:, h, :], in0=attT[:, h, :], scalar=gaT[:, h:h + 1], in1=rstda,
                op0=ALU.mult, op1=ALU.mult),
                reads=[("att", h, i) for i in range(NQT)] + [("rstda", n_) for n_ in range(3)] + ["par"],
                writes=[("mix", h, p) for p in range(3)])

        if _STAGE and _STAGE["name"] == "ATT":
            dump(arena[:, o_mix:o_mix + _STAGE["n"]], _STAGE["n"])

        S.full_barrier()
        cur[0] = base
        o_x1 = alloc(KC * T)
        x1T = f32v(o_x1, KC * T).rearrange("p (c t) -> p c t", c=KC)
        o_h2 = alloc(KC * T // 2)
        h2T = bfv(o_h2, KC * T).rearrange("p (c t) -> p c t", c=KC)
        ffn_base = cur[0]
        o_wo = [alloc(KC * 256 // 2) for _ in range(2)]
        wo = [bfv(o, KC * 256).rearrange("p (c n) -> p c n", c=KC) for o in o_wo]
        o_xr = [alloc(T) for _ in range(2)]
        xr = [f32v(o, T) for o in o_xr]
        o_sq2 = [alloc(T // 2) for _ in range(2)]
        sq2 = [bfv(o, T) for o in o_sq2]
        o_r2 = alloc(T)
        rstd2 = f32v(o_r2, T)

        def load_wo(n_):
            sl = n_ % 2
            S.add("pool", lambda e: e.dma_start(
                out=wo[sl], in_=w_out[:, n_ * 256:(n_ + 1) * 256].rearrange("(c p) n -> p c n", p=128)),
                writes=[("wo", sl)], dma="wo%d" % sl)

        load_wo(0)
        load_wo(1)
        for m in range(KC):
            pair = m // 2
            sl = pair % 2
            off = (m % 2) * 128
            bb = 3 * (m % 2)
            xs = m % 2
            S.add("sp", lambda e, m=m, xs=xs: e.dma_start(out=xr[xs], in_=xh[m * 128:(m + 1) * 128, QOFF:QOFF + T]),
                  writes=[("xr", xs)], dma="xr%d" % xs)
            for n_, (lo, hi) in enumerate(TT):
                for k in range(KC):
                    S.add("pe", lambda e, k=k, sl=sl, off=off, bb=bb, n_=n_, lo=lo, hi=hi: e.matmul(
                        ps[:, bb + n_, 0:342], lhsT=wo[sl][:, k, off:off + 128], rhs=mixT[:, k, lo:hi],
                        start=(k == 0), stop=(k == KC - 1)),
                        reads=[("wo", sl), ("mix", k, n_)], writes=[("ps", bb + n_)])
            for n_, (lo, hi) in enumerate(TT):
                S.add("dve", lambda e, m=m, bb=bb, n_=n_, lo=lo, hi=hi, xs=xs: e.tensor_tensor(
                    out=x1T[:, m, lo:hi], in0=ps[:, bb + n_, 0:342], in1=xr[xs][:, lo:hi], op=ALU.add),
                    reads=[("ps", bb + n_), ("xr", xs)], writes=[("x1", m, n_)])
            if m % 2 == 1 and pair + 2 < 8:
                load_wo(pair + 2)

        def rms_stats(src_fn, ncols, tiles, sqbuf, key_src_fn, outr, outkey, nfeat):
            for c in range(KC):
                sl = c % 2
                S.add("dve", lambda e, c=c, sl=sl: e.tensor_tensor(out=sqbuf[sl][:, 0:ncols], in0=src_fn(c), in1=src_fn(c), op=ALU.mult),
                      reads=key_src_fn(c), writes=[("sqb", sl)])
                for n_, (lo, hi) in enumerate(tiles):
                    S.add("pe", lambda e, c=c, sl=sl, n_=n_, lo=lo, hi=hi: e.matmul(
                        ps[:, n_, 0:hi - lo], lhsT=onesb, rhs=sqbuf[sl][:, lo:hi], start=(c == 0), stop=(c == KC - 1)),
                        reads=[("sqb", sl), "onesb"], writes=[("ps", n_)])
            for n_, (lo, hi) in enumerate(tiles):
                rstd_from(outr[:, lo:hi], ps[:, n_, 0:hi - lo], nfeat, [("ps", n_)], [(outkey, n_)])

        rms_stats(lambda c: x1T[:, c, :], T, TT, sq2, lambda c: [("x1", c, n_) for n_ in range(3)], rstd2, "rstd2", float(D))
        for c in range(KC):
            S.add("dve", lambda e, c=c: e.scalar_tensor_tensor(
                out=h2T[:, c, :], in0=x1T[:, c, :], scalar=g2T[:, c:c + 1], in1=rstd2, op0=ALU.mult, op1=ALU.mult),
                reads=[("x1", c, n_) for n_ in range(3)] + [("rstd2", n_) for n_ in range(3)] + ["par"],
                writes=[("h2", c)])
        S.add("dve", lambda e: e.tensor_scalar(out=h2T[:, :, 0:1], in0=h2T[:, :, 0:1], scalar1=qv[:, 0:1], scalar2=None,
                                               op0=ALU.mult),
              reads=[("h2", c) for c in range(KC)] + ["par"], writes=[("h2", c) for c in range(KC)])
        S.add("dve", lambda e: e.tensor_scalar(out=h2T[:, :, T - 1:T], in0=h2T[:, :, T - 1:T], scalar1=qv[:, 1:2],
                                               scalar2=None, op0=ALU.mult),
              reads=[("h2", c) for c in range(KC)] + ["par"], writes=[("h2", c) for c in range(KC)])

        if _STAGE and _STAGE["name"] == "X1":
            dump(arena[:, o_x1:o_x1 + _STAGE["n"]], _STAGE["n"])

        S.full_barrier()
        cur[0] = ffn_base
        GMAX = max(FGROUPS)
        o_wg = [alloc(KC * 256 // 2) for _ in range(2)]
        wg = [bfv(o, KC * 256).rearrange("p (c n) -> p c n", c=KC) for o in o_wg]
        o_wv = [alloc(KC * 256 // 2) for _ in range(2)]
        wv = [bfv(o, KC * 256).rearrange("p (c n) -> p c n", c=KC) for o in o_wv]
        o_act = [alloc(GMAX * TOK // 2) for _ in range(2)]
        actb = [bfv(o, GMAX * TOK).rearrange("p (f t) -> p f t", f=GMAX) for o in o_act]
        _save = cur[0]
        cur[0] = o_mix
        o_wd = [alloc(GMAX * 512 // 2) for _ in range(2)]
        wd = [bfv(o, GMAX * 512).rearrange("p (f n) -> p f n", f=GMAX) for o in o_wd]
        o_t1 = [alloc(342) for _ in range(3)]
        t1s = [f32v(o, 342) for o in o_t1]
        o_ge = [alloc(342) for _ in range(3)]
        ges = [f32v(o, 342) for o in o_ge]
        assert cur[0] <= o_mix + KC * T // 2
        cur[0] = _save
        GT = [(0, 344), (342, 686), (684, 1026)]
        OT = [(1, 343), (343, 685), (685, 1025)]

        npairs = (NFF + 1) // 2
        f_lo = {2: 0, 3: FSPLIT}.get(part, 0)
        pair_hi = {0: npairs, 2: FSPLIT // 2, 3: npairs}.get(part, 0)

        def load_wup(pi):
            if pi >= pair_hi:
                return
            sl = pi % 2
            f0 = pi * 2 - f_lo
            nf = min(2, NFF - pi * 2)
            S.add("pool", lambda e: e.dma_start(
                out=wg[sl][:, :, 0:nf * 128],
                in_=w_upg[:, f0 * 128:(f0 + nf) * 128].rearrange("(c p) n -> p c n", p=128)),
                writes=[("wg", sl)], dma="wg%d" % sl)
            S.add("pool", lambda e: e.dma_start(
                out=wv[sl][:, :, 0:nf * 128],
                in_=w_upv[:, f0 * 128:(f0 + nf) * 128].rearrange("(c p) n -> p c n", p=128)),
                writes=[("wv", sl)], dma="wv%d" % sl)

        nwd = 0

        def load_wd(f0, gsz, q):
            nonlocal nwd
            sl = nwd % 2
            nwd += 1
            S.add("pool", lambda e: e.dma_start(
                out=wd[sl][:, 0:gsz, :],
                in_=w_dn[(f0 - f_lo) * 128:(f0 - f_lo + gsz) * 128, q * 512:(q + 1) * 512].rearrange("(f p) n -> p f n", p=128)),
                writes=[("wd", sl)], dma="wd%d" % sl)
            return sl

        f = 0
        dcnt = 0
        assert sum(FGROUPS[:2]) == FSPLIT and FSPLIT % 2 == 0
        for gi, gsz in enumerate(FGROUPS):
            if gi == 0:
                S = S_real if part in (0, 2) else S_null
                load_wup(0)
                load_wup(1)
            if gi == 2:
                if part == 2:
                    S.full_barrier()
                    S.add("sp", lambda e: e.dma_start(out=x1_o, in_=arena[:, o_x1:o_x1 + KC * T]), dma="x1o")
                    S.add("sp", lambda e: e.dma_start(out=h2_o, in_=arena[:, o_h2:o_h2 + KC * T // 2].bitcast(BF16)), dma="h2o")
                    S.full_barrier()
                    S.add("sp", lambda e: None)
                S = S_real if part in (0, 3) else S_null
                if part == 3:
                    S.add("sp", lambda e: e.dma_start(out=arena[:, o_x1:o_x1 + KC * T], in_=x1_i),
                          writes=[("x2", m, n_) for m in range(KC) for n_ in range(2)], dma="x1i")
                    S.add("sp", lambda e: e.dma_start(out=arena[:, o_h2:o_h2 + KC * T // 2].bitcast(BF16), in_=h2_i),
                          writes=[("h2", c) for c in range(KC)], dma="h2i")
                    S.full_barrier()
                    load_wup(FSPLIT // 2)
                    load_wup(FSPLIT // 2 + 1)
            asl = gi % 2
            f0g = f
            wd_pre = (load_wd(f0g, gsz, 0), load_wd(f0g, gsz, 1))
            for fi in range(gsz):
                pi = f // 2
                sl = pi % 2
                off = (f % 2) * 128
                for n_, (lo, hi) in enumerate(GT):
                    for k in range(KC):
                        S.add("pe", lambda e, k=k, sl=sl, off=off, n_=n_, lo=lo, hi=hi: e.matmul(
                            ps[:, n_, 0:hi - lo], lhsT=wg[sl][:, k, off:off + 128], rhs=h2T[:, k, lo:hi],
                            start=(k == 0), stop=(k == KC - 1)),
                            reads=[("wg", sl), ("h2", k)], writes=[("ps", n_)])
                for n_, (lo, hi) in enumerate(OT):
                    for k in range(KC):
                        S.add("pe", lambda e, k=k, sl=sl, off=off, n_=n_, lo=lo, hi=hi: e.matmul(
                            ps[:, 3 + n_, 0:hi - lo], lhsT=wv[sl][:, k, off:off + 128], rhs=h2T[:, k, lo:hi],
                            start=(k == 0), stop=(k == KC - 1)),
                            reads=[("wv", sl), ("h2", k)], writes=[("ps", 3 + n_)])
                if f % 2 == 1 and pi + 2 < npairs:
                    load_wup(pi + 2)
                for n_, (lo, hi) in enumerate(OT):
                    w = hi - lo
                    S.add("act", lambda e, f=f, n_=n_, w=w: e.activation(
                        out=t1s[n_][:, 0:w], in_=ps[:, n_, 1:1 + w], func=AF.Identity,
                        scale=dwT[:, f, 1:2], bias=dwT[:, f, 3:4]),
                        reads=[("ps", n_), "par"], writes=[("t1", n_)])
                for n_, (lo, hi) in enumerate(OT):
                    w = hi - lo
                    S.add("dve", lambda e, f=f, n_=n_, w=w: e.scalar_tensor_tensor(
                        out=t1s[n_][:, 0:w], in0=ps[:, n_, 0:w], scalar=dwT[:, f, 0:1], in1=t1s[n_][:, 0:w],
                        op0=ALU.mult, op1=ALU.add),
                        reads=[("ps", n_), ("t1", n_), "par"], writes=[("t1", n_)])
                for n_, (lo, hi) in enumerate(OT):
                    w = hi - lo
                    S.add("dve", lambda e, f=f, n_=n_, w=w: e.scalar_tensor_tensor(
                        out=t1s[n_][:, 0:w], in0=ps[:, n_, 2:2 + w], scalar=dwT[:, f, 2:3], in1=t1s[n_][:, 0:w],
                        op0=ALU.mult, op1=ALU.add),
                        reads=[("ps", n_), ("t1", n_), "par"], writes=[("t1", n_)])
                for n_, (lo, hi) in enumerate(OT):
                    w = hi - lo
                    S.add("act", lambda e, n_=n_, w=w: e.activation(out=ges[n_][:, 0:w], in_=t1s[n_][:, 0:w], func=AF.Gelu),
                          reads=[("t1", n_)], writes=[("ge", n_)])
                for n_, (lo, hi) in enumerate(OT):
                    w = hi - lo
                    S.add("dve", lambda e, n_=n_, w=w, fi=fi, lo=lo, hi=hi, asl=asl: e.tensor_tensor(
                        out=actb[asl][:, fi, lo - 1:hi - 1], in0=ges[n_][:, 0:w], in1=ps[:, 3 + n_, 0:w], op=ALU.mult),
                        reads=[("ge", n_), ("ps", 3 + n_)], writes=[("act", asl, fi)])
                f += 1
            wsl, wsl1 = wd_pre
            for q in range(4):
                if q == 0:
                    nxt = wsl1
                else:
                    nxt = load_wd(f0g, gsz, q + 1) if q + 1 < 4 else None
                for mm in range(4):
                    m = q * 4 + mm
                    for n_ in range(2):
                        bk = 6 + (dcnt % 2)
                        dcnt += 1
                        for fi in range(gsz):
                            S.add("pe", lambda e, fi=fi, wsl=wsl, mm=mm, n_=n_, bk=bk, asl=asl, gsz=gsz: e.matmul(
                                ps[:, bk, 0:512], lhsT=wd[wsl][:, fi, mm * 128:(mm + 1) * 128],
                                rhs=actb[asl][:, fi, n_ * 512:(n_ + 1) * 512], start=(fi == 0), stop=(fi == gsz - 1)),
                                reads=[("wd", wsl), ("act", asl, fi)], writes=[("ps", bk)])
                        S.add("dve", lambda e, m=m, n_=n_, bk=bk: e.tensor_tensor(
                            out=x1T[:, m, 1 + n_ * 512:1 + (n_ + 1) * 512], in0=ps[:, bk, 0:512],
                            in1=x1T[:, m, 1 + n_ * 512:1 + (n_ + 1) * 512], op=ALU.add),
                            reads=[("ps", bk), ("x2", m, n_)], writes=[("x2", m, n_)])
                wsl = nxt

        if _STAGE and _STAGE["name"] == "X2":
            dump(arena[:, o_x1:o_x1 + _STAGE["n"]], _STAGE["n"])

        S.full_barrier()
        cur[0] = ffn_base
        o_sq3 = [alloc(TOK // 2) for _ in range(2)]
        sq3 = [bfv(o, TOK) for o in o_sq3]
        o_r3 = alloc(TOK)
        rstd3 = f32v(o_r3, TOK)
        o_yo = [alloc(TOK) for _ in range(2)]
        yo = [f32v(o, TOK) for o in o_yo]
        T2 = [(0, 512), (512, 1024)]
        rms_stats(lambda c: x1T[:, c, 1:1 + TOK], TOK, T2, sq3,
                  lambda c: [("x2", c, 0), ("x2", c, 1)], rstd3, "rstd3", float(D))
        outs = []
        for c in range(KC):
            sl = c % 2
            S.add("dve", lambda e, c=c, sl=sl: e.scalar_tensor_tensor(
                out=yo[sl], in0=x1T[:, c, 1:1 + TOK], scalar=gnT[:, c:c + 1], in1=rstd3, op0=ALU.mult, op1=ALU.mult),
                reads=[("x2", c, 0), ("x2", c, 1), ("rstd3", 0), ("rstd3", 1), "par"], writes=[("yo", sl)])
            outs.append(S.add("sp", lambda e, c=c, sl=sl: e.dma_start(out=yT[c * 128:(c + 1) * 128, :], in_=yo[sl]),
                              reads=[("yo", sl)], dma="yo%d" % sl))
        S.full_barrier()
        S.add("sp", lambda e: None)

        def sem_alloc(name):
            return sem_pool.pop()

        S_real.emit(nc, block, sem_alloc)
    return nc


def _alibi_slopes(n_heads):
    def pow2_slopes(n):
        start = 2.0 ** (-8.0 / n)
        return [start ** (i + 1) for i in range(n)]
    if math.log2(n_heads).is_integer():
        s = pow2_slopes(n_heads)
    else:
        closest = 2 ** int(math.floor(math.log2(n_heads)))
        s = pow2_slopes(closest) + pow2_slopes(2 * closest)[0::2][: n_heads - closest]
    return np.array(s, dtype=np.float32)


_CONST_CACHE = {}


def _constants():
    if _CONST_CACHE:
        return _CONST_CACHE
    bf = ml_dtypes.bfloat16
    slopes = _alibi_slopes(NH)
    kk = np.arange(QT_)[:, None, None]
    r = np.arange(5)[None, :, None]
    qq = np.arange(QT_)[None, None, :]
    rel = QT_ * (r - 2) + kk - qq
    ab = np.empty((QT_, NH, 5, QT_), np.float32)
    for h in range(NH):
        ab[:, h] = np.where(np.abs(rel) <= 128, -slopes[h] * np.abs(rel).astype(np.float32), -30000.0)
    _CONST_CACHE["abias"] = ab.reshape(QT_, NH * 5 * QT_)
    k = np.arange(128)
    ang = 2.0 * np.pi * ((k[:, None] * k[None, :]) % 128) / 128.0
    cc = np.concatenate([np.cos(ang) / 1024.0, np.sin(ang) / 1024.0], axis=1)
    _CONST_CACHE["cc128"] = cc.astype(np.float32).astype(bf)
    s = np.arange(SEQ, dtype=np.int64)[:, None]
    tabs = []
    for c in range(NCORES):
        t = (np.arange(T, dtype=np.int64) + 1024 * c - 1) % SEQ
        ang = 2.0 * np.pi * ((s * t[None, :]) % SEQ).astype(np.float64) / SEQ
        co = np.cos(ang).astype(np.float32).astype(bf)
        si = np.sin(ang).astype(np.float32).astype(bf)
        tb = np.empty((3, SEQ, 2 * 342), bf)
        for p, (lo, hi) in enumerate(TT):
            tb[p, :, 0:342] = co[:, lo:hi]
            tb[p, :, 342:684] = si[:, lo:hi]
        tabs.append(tb)
    _CONST_CACHE["tab"] = tabs
    cst = np.zeros((128, 256), np.float32)
    cst[:, 0:128] = np.eye(128, dtype=np.float32)
    _CONST_CACHE["cst"] = cst
    return _CONST_CACHE


def _colmajor(v, n):
    return np.ascontiguousarray(np.asarray(v, np.float32).reshape(n, 128).T)


def _prep(inputs):
    C = _constants()
    x = np.asarray(inputs["x"], np.float32)[0]
    xT = np.ascontiguousarray(x.T)
    w_in = np.asarray(inputs["w_in"], np.float32)
    w_up = np.asarray(inputs["w_up"], np.float32)
    w_down = np.asarray(inputs["w_down"], np.float32)
    par0 = np.zeros((128, 320), np.float32)
    par0[:, 0:16] = _colmajor(inputs["norm1_g"], 16)
    par0[:, 16:32] = _colmajor(inputs["norm2_g"], 16)
    par0[:, 32:48] = _colmajor(inputs["normf_g"], 16)
    par0[:, 48:60] = _colmajor(inputs["attn_out_g"], 12)
    par0[:, 60:64] = _colmajor(inputs["fourier_out_g"], 4)
    par0[:, 64:76] = np.asarray(inputs["sink"], np.float32)[None, :]
    dw = np.asarray(inputs["dw_w"], np.float32)[:, 0, :]
    db = np.asarray(inputs["dw_b"], np.float32)
    dwp = np.zeros((128, 43, 4), np.float32)
    for k in range(3):
        dwp[:, :, k] = _colmajor(dw[k], 43)
    dwp[:, :, 3] = _colmajor(db, 43)
    par0[:, 96:96 + 172] = dwp.reshape(128, 172)
    fs = FSPLIT * 128
    sh = {
        "xT": xT,
        "w_u": np.ascontiguousarray(w_in[:, 2560:3072]),
        "w_qkv": np.ascontiguousarray(w_in[:, 0:2560]),
        "w_out": np.ascontiguousarray(np.asarray(inputs["w_out"], np.float32)),
        "w_fou": np.ascontiguousarray(np.asarray(inputs["w_fourier"], np.float32)),
        "upg2": np.ascontiguousarray(w_up[:, 0:fs]), "upv2": np.ascontiguousarray(w_up[:, DFF:DFF + fs]),
        "dn2": np.ascontiguousarray(w_down[0:fs, :]),
        "upg3": np.ascontiguousarray(w_up[:, fs:DFF]), "upv3": np.ascontiguousarray(w_up[:, DFF + fs:2 * DFF]),
        "dn3": np.ascontiguousarray(w_down[fs:, :]),
        "upg": np.ascontiguousarray(w_up[:, 0:DFF]), "upv": np.ascontiguousarray(w_up[:, DFF:2 * DFF]),
        "dn": np.ascontiguousarray(w_down),
    }
    pars, xhs = [], []
    for c in range(NCORES):
        t0 = 1024 * c - 1
        tok = t0 - QOFF + np.arange(XH)
        valid = (tok >= 0) & (tok < SEQ)
        xhc = np.zeros((D, XH), np.float32)
        xhc[:, valid] = xT[:, tok[valid]]
        par = par0.copy()
        par[0:QT_, 76:89] = valid.astype(np.float32).reshape(NKT, QT_).T
        par[:, 89] = 1.0 if t0 >= 0 else 0.0
        par[:, 90] = 1.0 if t0 + T - 1 < SEQ else 0.0
        pars.append(par)
        xhs.append(xhc)
    return C, sh, pars, xhs


def _maps1(C, sh, pars, xhs, c):
    return {"xT": sh["xT"], "w_u": sh["w_u"], "w_fou": sh["w_fou"], "tab": C["tab"][c], "cc128": C["cc128"],
            "par": pars[c], "cst": C["cst"]}


def _maps2(C, sh, pars, xhs, c, fT):
    return {"xh": xhs[c], "w_qkv": sh["w_qkv"], "w_out": sh["w_out"], "w_upg": sh["upg2"], "w_upv": sh["upv2"],
            "w_dn": sh["dn2"], "abias": C["abias"], "par": pars[c], "cst": C["cst"], "fT": fT}


def _maps3(C, sh, pars, xhs, c, x1, h2):
    return {"w_upg": sh["upg3"], "w_upv": sh["upv3"], "w_dn": sh["dn3"], "par": pars[c], "cst": C["cst"],
            "x1i": x1, "h2i": h2}


_NC_CACHE = {}


def _prog(part):
    if part not in _NC_CACHE:
        _NC_CACHE[part] = build_program(part)
    return _NC_CACHE[part]


FUSED = True


def _maps0(C, sh, pars, xhs, c):
    return {"xT": sh["xT"], "w_u": sh["w_u"], "w_fou": sh["w_fou"], "tab": C["tab"][c], "cc128": C["cc128"],
            "par": pars[c], "cst": C["cst"], "xh": xhs[c], "w_qkv": sh["w_qkv"], "w_out": sh["w_out"],
            "w_upg": sh["upg"], "w_upv": sh["upv"], "w_dn": sh["dn"], "abias": C["abias"]}


def kernel(**inputs):
    C, sh, pars, xhs = _prep(inputs)
    cores = list(range(NCORES))
    if FUSED:
        r = run_bass_kernel_spmd(_prog(0), [_maps0(C, sh, pars, xhs, c) for c in cores], core_ids=cores).results
        out = np.empty((1, SEQ, D), np.float32)
        for c in cores:
            out[0, c * TOK:(c + 1) * TOK, :] = np.asarray(r[c]["yT"], np.float32).T
        return out
    r1 = run_bass_kernel_spmd(_prog(1), [_maps1(C, sh, pars, xhs, c) for c in cores], core_ids=cores).results
    r2 = run_bass_kernel_spmd(_prog(2), [_maps2(C, sh, pars, xhs, c, np.asarray(r1[c]["fT"])) for c in cores],
                              core_ids=cores).results
    r3 = run_bass_kernel_spmd(_prog(3), [_maps3(C, sh, pars, xhs, c, np.asarray(r2[c]["x1o"]), np.asarray(r2[c]["h2o"]))
                                         for c in cores], core_ids=cores).results
    out = np.empty((1, SEQ, D), np.float32)
    for c in cores:
        out[0, c * TOK:(c + 1) * TOK, :] = np.asarray(r3[c]["yT"], np.float32).T
    return out
```

```python
import math
from contextlib import ExitStack
import numpy as np
import ml_dtypes
import concourse.bass as bass
import concourse.mybir as mybir
from concourse.bass_utils import run_bass_kernel_spmd

F32 = mybir.dt.float32
BF16 = mybir.dt.bfloat16
ALU = mybir.AluOpType
AF = mybir.ActivationFunctionType
AX = mybir.AxisListType

NCORES = 8
D = 2048
KC = 16
SEQ = 8192
TOK = 1024
T = 1026
TT = [(0, 342), (342, 684), (684, 1026)]
QT_ = 114
NQT = 9
NKT = 13
XH = NKT * QT_
QOFF = 2 * QT_
NH = 12
NKV = 4
DFF = 5504
NFF = 43
EPS = 1e-6
FGROUPS = [9, 9, 9, 8, 8]
ARENA = 52480

_STAGE = None
_DBG = {}


class Op:
    __slots__ = ("eng", "fn", "deps", "dma", "sig", "val", "sem", "idx")

    def __init__(self, eng, fn, deps, dma):
        self.eng, self.fn, self.deps, self.dma = eng, fn, deps, dma
        self.sig = False
        self.val = None
        self.sem = None


class Sched:
    ENGS = ("pe", "act", "dve", "pool", "sp")

    def __init__(self):
        self.ops = {e: [] for e in self.ENGS}
        self.lastw = {}
        self.readers = {}
        self.barrier = {}
        self.nops = 0

    def add(self, eng, fn, reads=(), writes=(), dma=None, extra=()):
        deps = []
        for k in reads:
            w = self.lastw.get(k)
            if w is not None:
                deps.append(w)
        for k in writes:
            w = self.lastw.get(k)
            if w is not None and (w.eng != eng or eng != "pe" or w.dma is not None or dma is not None):
                deps.append(w)
            for r in self.readers.get(k, {}).values():
                if r.eng != eng or eng != "pe" or r.dma is not None or dma is not None:
                    deps.append(r)
        deps.extend(self.barrier.values())
        deps.extend(extra)
        seen = set()
        dd = []
        for d in deps:
            if id(d) not in seen:
                seen.add(id(d))
                dd.append(d)
        op = Op(eng, fn, dd, dma)
        op.idx = self.nops
        self.nops += 1
        for d in dd:
            d.sig = True
        self.ops[eng].append(op)
        for k in reads:
            self.readers.setdefault(k, {})[(eng, dma)] = op
        for k in writes:
            self.lastw[k] = op
            self.readers[k] = {}
        return op

    def full_barrier(self):
        b = {}
        for e in self.ENGS:
            if self.ops[e]:
                b[e] = self.ops[e][-1]
        lastdma = {}
        for e in self.ENGS:
            for op in self.ops[e]:
                if op.dma is not None:
                    lastdma[op.dma] = op
        for ch, op in lastdma.items():
            b[("dma", ch)] = op
        self.barrier = b

    def emit(self, nc, block, sem_alloc):
        eng_sem = {e: sem_alloc("s_" + e) for e in self.ENGS}
        chan_sem = {}
        chan_cnt = {}
        for e in self.ENGS:
            cnt = 0
            for op in self.ops[e]:
                if op.dma is not None:
                    if op.dma not in chan_sem:
                        chan_sem[op.dma] = sem_alloc("d_" + op.dma)
                        chan_cnt[op.dma] = 0
                    chan_cnt[op.dma] += 16
                    op.val = chan_cnt[op.dma]
                    op.sem = chan_sem[op.dma]
                elif op.sig:
                    cnt += 1
                    op.val = cnt
                    op.sem = eng_sem[e]

        def runner(ename):
            def run(e):
                waited = {}
                for op in self.ops[ename]:
                    for d in op.deps:
                        key = id(d.sem)
                        if waited.get(key, 0) < d.val:
                            e.wait_ge(d.sem, d.val)
                            waited[key] = d.val
                    ins = op.fn(e)
                    if ins is None:
                        if not op.sig:
                            continue
                        ins = e.nop()
                    if op.dma is not None:
                        ins.then_inc(op.sem, 16)
                    elif op.sig:
                        ins.then_inc(op.sem, 1)
            return run

        block.tensor(runner("pe"))
        block.scalar(runner("act"))
        block.vector(runner("dve"))
        block.gpsimd(runner("pool"))
        block.sync(runner("sp"))


FSPLIT = 18


class LimitSched:
    def __init__(self, real, n):
        self.real, self.n = real, n

    def add(self, *a, **k):
        if self.n <= 0:
            return None
        self.n -= 1
        return self.real.add(*a, **k)

    def full_barrier(self):
        self.real.full_barrier()


class NullSched:
    def add(self, *a, **k):
        return None

    def full_barrier(self):
        pass


def build_program(part):
    nc = bass.Bass("TRN2", target_bir_lowering=False)

    def din(name, shape, dt=F32, parts=(1, 2, 3)):
        if part != 0 and part not in parts:
            return None
        return nc.dram_tensor(name, list(shape), dt, kind="ExternalInput").ap()

    def dout(name, shape, dt, parts):
        if part not in parts and not (part == 0 and name == "yT"):
            return None
        return nc.dram_tensor(name, list(shape), dt, kind="ExternalOutput").ap()

    NB = _DBG.get("nb", 16)
    NQ = _DBG.get("nq", 16)
    xT = din("xT", [D, 512 * NB], parts=(1,))
    xh = din("xh", [D, XH], parts=(2,))
    w_u = din("w_u", [D, 512], parts=(1,))
    w_qkv = din("w_qkv", [D, 2560], parts=(2,))
    w_out = din("w_out", [D, D], parts=(2,))
    nf_part = {0: NFF, 2: FSPLIT, 3: NFF - FSPLIT}.get(part, 1)
    w_upg = din("w_upg", [D, nf_part * 128], parts=(2, 3))
    w_upv = din("w_upv", [D, nf_part * 128], parts=(2, 3))
    w_dn = din("w_dn", [nf_part * 128, D], parts=(2, 3))
    w_fou = din("w_fou", [4, 128, 128], parts=(1,))
    tab = din("tab", [3, 512 * NQ, 2 * 342], BF16, parts=(1,))
    cc128 = din("cc128", [128, 2 * 128], BF16, parts=(1,))
    abias = din("abias", [QT_, NH * 5 * QT_], parts=(2,))
    par = din("par", [128, 320])
    cst = din("cst", [128, 256])
    fT_o = dout("fT", [128, 4 * T], BF16, (1,))
    fT_i = din("fT", [128, 4 * T], BF16, parts=(2,)) if part != 0 else None
    x1_o = dout("x1o", [128, KC * T], F32, (2,))
    h2_o = dout("h2o", [128, KC * T], BF16, (2,))
    x1_i = din("x1i", [128, KC * T], F32, parts=(3,)) if part != 0 else None
    h2_i = din("h2i", [128, KC * T], BF16, parts=(3,)) if part != 0 else None
    yT = dout("yT", [D, TOK], F32, (3,))
    dbg = None
    if _STAGE is not None:
        dbg = nc.dram_tensor("dbg", [128, _STAGE["n"]], F32, kind="ExternalOutput").ap()

    S_real = Sched()
    S_null = NullSched()
    S = S_real

    with ExitStack() as stack:
        arena = stack.enter_context(nc.sbuf_tensor("arena", [128, ARENA], F32))
        ps = stack.enter_context(nc.psum_tensor("ps", [128, 8, 512], F32))
        sem_pool = [stack.enter_context(nc.semaphore("sm%d" % i)) for i in range(56)]
        block = stack.enter_context(nc.Block())
        def f32v(off, n):
            return arena[:, off:off + n]

        def bfv(off, n_bf):
            assert n_bf % 2 == 0
            return arena[:, off:off + n_bf // 2].bitcast(BF16)

        cur = [0]

        def alloc(nwords):
            o = cur[0]
            cur[0] += nwords
            assert cur[0] <= ARENA, (cur[0], ARENA)
            return o

        o_par = alloc(320)
        P_ = f32v(o_par, 320)
        g1T = P_[:, 0:16]
        g2T = P_[:, 16:32]
        gnT = P_[:, 32:48]
        gaT = P_[:, 48:60]
        gfoT = P_[:, 60:64]
        sinkB = P_[:, 64:76]
        kvalid = P_[:, 76:89]
        qv = P_[:, 89:91]
        dwT = P_[:, 96:96 + 43 * 4].rearrange("p (f k) -> p f k", k=4)
        o_es = alloc(12)
        es = f32v(o_es, 12)
        o_eps = alloc(2)
        epsT = f32v(o_eps, 1)
        o_id = alloc(128)
        identf = f32v(o_id, 128)
        o_idb = alloc(64)
        identb = bfv(o_idb, 128)
        o_on = alloc(64)
        onesb = bfv(o_on, 128)
        o_vo = alloc(NKT * 64)
        vones = bfv(o_vo, NKT * 128).rearrange("p (j m) -> p j m", j=NKT)
        o_mix = alloc(KC * T // 2)
        mixT = bfv(o_mix, KC * T).rearrange("p (c t) -> p c t", c=KC)
        base = cur[0]

        ld_par = S.add("sp", lambda e: e.dma_start(out=P_, in_=par), writes=["par"], dma="par")
        ld_id = S.add("sp", lambda e: e.dma_start(out=identf, in_=cst[:, 0:128]), writes=["identf"], dma="cst")
        S.add("dve", lambda e: e.tensor_copy(out=identb, in_=identf), reads=["identf"], writes=["identb"])
        S.add("dve", lambda e: e.memset(onesb, 1.0), writes=["onesb"])
        S.add("dve", lambda e: e.memset(epsT, EPS), writes=["epsT"])
        S.add("act", lambda e: e.activation(out=es, in_=sinkB, func=AF.Exp), reads=["par"], writes=["es"])
        for j in range(NKT):
            S.add("dve", lambda e, j=j: e.tensor_scalar(out=vones[:, j, :], in0=onesb, scalar1=kvalid[:, j:j + 1],
                                                         scalar2=None, op0=ALU.mult),
                  reads=["par", "onesb"], writes=[("vones", j)])

        def rstd_from(out_ap, in_ap, n, rkeys, wkeys):
            S.add("act", lambda e: e.activation(out=out_ap, in_=in_ap, func=AF.Ln, scale=1.0 / n, bias=epsT),
                  reads=list(rkeys) + ["epsT"], writes=wkeys)
            S.add("act", lambda e: e.activation(out=out_ap, in_=out_ap, func=AF.Exp, scale=-0.5), reads=wkeys, writes=wkeys)

        def dump(ap_f32_view, n):
            S.full_barrier()
            S.add("sp", lambda e: e.dma_start(out=dbg[:, 0:n], in_=ap_f32_view), dma="dbg")
            S.full_barrier()
            S.add("sp", lambda e: None)

        o_E = ARENA - (NH * 570 // 2 + 2 * 570)
        Eb = bfv(o_E, NH * 570).rearrange("p (h r q) -> p h r q", h=NH, r=5)
        abs_ = [f32v(o_E + NH * 570 // 2 + k * 570, 570) for k in range(2)]
        S = S_real if part in (0, 1) else S_null
        cur[0] = base
        o_U = alloc(64 * 512 // 2)
        U = bfv(o_U, 64 * 512).rearrange("p (i n) -> p i n", i=64)
        f_base = cur[0]
        o_wus = alloc(KC * 512)
        wus = f32v(o_wus, KC * 512).rearrange("p (c n) -> p c n", c=KC)
        o_wub = alloc(KC * 512 // 2)
        wub = bfv(o_wub, KC * 512).rearrange("p (c n) -> p c n", c=KC)
        o_xb = [alloc(KC * 512 // 2) for _ in range(2)]
        xb = [bfv(o, KC * 512).rearrange("p (c n) -> p c n", c=KC) for o in o_xb]
        o_sm = alloc(64 * 2)
        ssu = f32v(o_sm, 64)
        rsu = f32v(o_sm + 64, 64)
        o_dg = [alloc(128) for _ in range(2)]
        dgt = [f32v(o, 128) for o in o_dg]

        for q4 in range(4):
            S.add("sp", lambda e, q4=q4: e.dma_start(
                out=wus[:, 4 * q4:4 * q4 + 4, :],
                in_=w_u[512 * q4:512 * (q4 + 1), :].rearrange("(c p) n -> p c n", p=128)),
                writes=[("wus", q4)], dma="wus%d" % q4)
        for c in range(KC):
            fold_last = S.add("dve", lambda e, c=c: e.tensor_scalar(out=wub[:, c, :], in0=wus[:, c, :], scalar1=g1T[:, c:c + 1],
                                                                     scalar2=None, op0=ALU.mult),
                              reads=[("wus", c // 4), "par"], writes=[("wub", c)])

        xstg = [wus[:, :, 0:256], wus[:, :, 256:512]]

        def load_xb(b):
            sl = b % 2
            for hf in range(2):
                S.add("sp", lambda e, hf=hf: e.dma_start(
                    out=xstg[hf], in_=xT[:, b * 512 + hf * 256:b * 512 + (hf + 1) * 256].rearrange("(c p) n -> p c n", p=128)),
                    reads=[], writes=[("xstg", hf)], dma="xs%d" % hf,
                    extra=([fold_last] if (b == 0 and fold_last is not None) else []))
                if hf == 0:
                    S.add("dve", lambda e, hf=hf: e.tensor_copy(out=xb[sl][:, :, 0:256], in_=xstg[0]),
                          reads=[("xstg", 0)], writes=[("xb", sl)])
                else:
                    S.add("act", lambda e, hf=hf: e.activation(out=xb[sl][:, :, 256:512], in_=xstg[1], func=AF.Copy),
                          reads=[("xstg", 1)], writes=[("xbh", sl)])

        if "F1" in _DBG.get("skip", ()):
            S = S_null
        load_xb(0)
        def e_build(h):
            sl = h % 2
            S_real.add("pool", lambda e: e.dma_start(out=abs_[sl][0:QT_, :], in_=abias[:, h * 570:(h + 1) * 570]),
                       writes=[("abs", sl)], dma="abs%d" % sl)
            S_real.add("act", lambda e: e.activation(
                out=Eb[0:QT_, h, :, :], in_=abs_[sl][0:QT_, :].rearrange("p (r q) -> p r q", r=5), func=AF.Exp),
                reads=[("abs", sl)], writes=[("E", h)])

        e_done = set()
        for b in range(NB):
            if part == 0 and 2 <= b < 2 + NH:
                e_build(b - 2)
                e_done.add(b - 2)
            sl = b % 2
            for i4 in range(4):
                if i4 == 2 and b + 1 < NB:
                    load_xb(b + 1)
                i = b * 4 + i4
                pb = i % 2
                cols = slice(i4 * 128, (i4 + 1) * 128)
                for c in range(KC):
                    S.add("pe", lambda e, c=c, sl=sl, cols=cols, pb=pb: e.matmul(
                        ps[:, pb, 0:512], lhsT=xb[sl][:, c, cols], rhs=wub[:, c, :], start=(c == 0), stop=(c == KC - 1)),
                        reads=[(("xb", sl) if i4 < 2 else ("xbh", sl)), ("wub", c)], writes=[("ps", pb)])
                for c in range(KC):
                    S.add("pe", lambda e, c=c, sl=sl, cols=cols, pb=pb: e.matmul(
                        ps[:, 2 + pb, 0:128], lhsT=xb[sl][:, c, cols], rhs=xb[sl][:, c, cols],
                        start=(c == 0), stop=(c == KC - 1)),
                        reads=[(("xb", sl) if i4 < 2 else ("xbh", sl))], writes=[("ps", 2 + pb)])
                S.add("dve", lambda e, pb=pb: e.tensor_tensor(out=dgt[pb], in0=ps[:, 2 + pb, 0:128], in1=identf, op=ALU.mult),
                      reads=[("ps", 2 + pb), "identf"], writes=[("dgt", pb)])
                S.add("dve", lambda e, pb=pb, i=i: e.reduce_sum(out=ssu[:, i:i + 1], in_=dgt[pb], axis=AX.X),
                      reads=[("dgt", pb)], writes=[("ssu", i)])
                rstd_from(rsu[:, i:i + 1], ssu[:, i:i + 1], float(D), [("ssu", i)], [("rsu", i)])
                S.add("act", lambda e, pb=pb, i=i: e.activation(out=U[:, i, :], in_=ps[:, pb, 0:512], func=AF.Copy,
                                                                 scale=rsu[:, i:i + 1]),
                      reads=[("ps", pb), ("rsu", i)], writes=[("U", i)])

        if _STAGE and _STAGE["name"] == "U":
            dump(arena[:, o_U:o_U + _STAGE["n"]], _STAGE["n"])

        S = S_real if part in (0, 1) else S_null
        if "F2" in _DBG.get("skip", ()):
            S = S_null
        S.full_barrier()
        cur[0] = f_base
        o_tab = [alloc(4 * 684 // 2) for _ in range(3)]
        tabs = [bfv(o, 4 * 684).rearrange("p (k n) -> p k n", k=4) for o in o_tab]
        o_AB = alloc(4 * 2 * T // 2)
        ABT = bfv(o_AB, 4 * 2 * T).rearrange("p (g a t) -> p g a t", g=4, a=2)
        o_ys = alloc(4 * T)
        ysb = f32v(o_ys, 4 * T).rearrange("p (g t) -> p g t", g=4)
        o_sqf = alloc(4 * T // 2)
        sqf = bfv(o_sqf, 4 * T).rearrange("p (g t) -> p g t", g=4)
        o_rf = alloc(T)
        rstdf = f32v(o_rf, T)
        o_mc = alloc(4 * 2 * 128 // 2)
        Mc = bfv(o_mc, 4 * 2 * 128).rearrange("p (g a n) -> p g a n", g=4, a=2)
        o_wf = alloc(4 * 128 // 2)
        wfb = bfv(o_wf, 4 * 128).rearrange("p (g n) -> p g n", g=4)
        o_cc = alloc(2 * 128 // 2)
        ccb = bfv(o_cc, 2 * 128).rearrange("p (a n) -> p a n", a=2)


        ntab = 0

        def load_tab(p, q):
            nonlocal ntab
            sl = ntab % 3
            ntab += 1
            S.add("sp", lambda e: e.dma_start(
                out=tabs[sl], in_=tab[p, q * 512:(q + 1) * 512, :].rearrange("(k s) n -> s k n", s=128)),
                writes=[("tab", sl)], dma="tab%d" % sl)
            return sl

        seq = [(p, q) for p in range(3) for q in range(NQ)]
        slots = {}
        slots[0] = load_tab(*seq[0])
        slots[1] = load_tab(*seq[1])
        for n_, (p, q) in enumerate(seq):
            if n_ + 2 < len(seq):
                slots[n_ + 2] = load_tab(*seq[n_ + 2])
            sl = slots[n_]
            lo, hi = TT[p]
            for k in range(4):
                i = q * 4 + k
                for g in range(4):
                    for a in range(2):
                        bk = g * 2 + a
                        S.add("pe", lambda e, i=i, g=g, a=a, bk=bk, sl=sl, k=k: e.matmul(
                            ps[:, bk, 0:342], lhsT=U[:, i, g * 128:(g + 1) * 128], rhs=tabs[sl][:, k, a * 342:(a + 1) * 342],
                            start=(i == 0), stop=(i == 4 * NQ - 1)),
                            reads=[("U", i), ("tab", sl)], writes=[("ps", bk)])
            if q == NQ - 1:
                for g in range(4):
                    for a in range(2):
                        bk = g * 2 + a
                        eng = "act" if a == 0 else "dve"
                        if eng == "act":
                            S.add("act", lambda e, g=g, a=a, bk=bk, lo=lo, hi=hi: e.activation(
                                out=ABT[:, g, a, lo:hi], in_=ps[:, bk, 0:342], func=AF.Copy),
                                reads=[("ps", bk)], writes=[("ABT", g, a, p)])
                        else:
                            S.add("dve", lambda e, g=g, a=a, bk=bk, lo=lo, hi=hi: e.tensor_copy(
                                out=ABT[:, g, a, lo:hi], in_=ps[:, bk, 0:342]),
                                reads=[("ps", bk)], writes=[("ABT", g, a, p)])

        S = S_real if part in (0, 1) else S_null
        if "F3" in _DBG.get("skip", ()):
            S = S_null
        if "f3n" in _DBG and part == 1:
            S = LimitSched(S_real, _DBG["f3n"])
        S.add("pool", lambda e: e.dma_start(out=wfb, in_=w_fou.rearrange("g k n -> k g n")), writes=["wfb"], dma="wfb")
        S.add("sp", lambda e: e.dma_start(out=ccb, in_=cc128.rearrange("p (a n) -> p a n", a=2)), writes=["ccb"], dma="ccb")
        for g in range(4):
            for a in range(2):
                S.add("pe", lambda e, g=g, a=a: e.matmul(ps[:, a, 0:128], lhsT=ccb[:, a, :], rhs=wfb[:, g, :],
                                                         start=True, stop=True),
                      reads=["ccb", "wfb"], writes=[("ps", a)])
                S.add("act", lambda e, g=g, a=a: e.activation(out=Mc[:, g, a, :], in_=ps[:, a, 0:128], func=AF.Copy,
                                                              scale=(1.0 if a == 0 else -1.0)),
                      reads=[("ps", a)], writes=[("Mc", g, a)])
        for p, (lo, hi) in enumerate(TT):
            for g in range(4):
                bk = 2 + (g % 2)
                for a in range(2):
                    S.add("pe", lambda e, g=g, a=a, bk=bk, lo=lo, hi=hi: e.matmul(
                        ps[:, bk, 0:342], lhsT=Mc[:, g, a, :], rhs=ABT[:, g, a, lo:hi], start=(a == 0), stop=(a == 1)),
                        reads=[("Mc", g, a), ("ABT", g, a, p)], writes=[("ps", bk)])
                S.add("dve", lambda e, g=g, bk=bk, lo=lo, hi=hi: e.tensor_copy(out=ysb[:, g, lo:hi], in_=ps[:, bk, 0:342]),
                      reads=[("ps", bk)], writes=[("ysb", g, p)])
                S.add("dve", lambda e, g=g, lo=lo, hi=hi: e.tensor_tensor(out=sqf[:, g, lo:hi], in0=ysb[:, g, lo:hi],
                                                                           in1=ysb[:, g, lo:hi], op=ALU.mult),
                      reads=[("ysb", g, p)], writes=[("sqf", g, p)])
            for g in range(4):
                S.add("pe", lambda e, g=g, lo=lo, hi=hi, p=p: e.matmul(ps[:, 4 + p, 0:342], lhsT=onesb, rhs=sqf[:, g, lo:hi],
                                                                       start=(g == 0), stop=(g == 3)),
                      reads=[("sqf", g, p), "onesb"], writes=[("ps", 4 + p)])
            rstd_from(rstdf[:, lo:hi], ps[:, 4 + p, 0:342], 512.0, [("ps", 4 + p)], [("rstdf", p)])
            for g in range(4):
                S.add("dve", lambda e, g=g, lo=lo, hi=hi: e.scalar_tensor_tensor(
                    out=mixT[:, 12 + g, lo:hi], in0=ysb[:, g, lo:hi], scalar=gfoT[:, g:g + 1], in1=rstdf[:, lo:hi],
                    op0=ALU.mult, op1=ALU.mult),
                    reads=[("ysb", g, p), ("rstdf", p), "par"], writes=[("mix", 12 + g, p)])

        if _STAGE and _STAGE["name"] == "F":
            dump(arena[:, o_mix:o_mix + _STAGE["n"]], _STAGE["n"])

        mixf = arena[:, o_mix + 12 * (T // 2):o_mix + 16 * (T // 2)].bitcast(BF16)
        if part == 1:
            S = S_real
            S.full_barrier()
            S.add("sp", lambda e: e.dma_start(out=fT_o, in_=mixf), dma="fTo")
            S.full_barrier()
            S.add("sp", lambda e: None)
        S = S_real if part in (0, 2) else S_null
        if part == 2:
            S.add("sp", lambda e: e.dma_start(out=mixf, in_=fT_i), writes=[("mix", 12 + g, p) for g in range(4) for p in range(3)],
                  dma="fTi")
        S.full_barrier()
        cur[0] = base
        o_QT = alloc(NH * T // 2)
        QTs = bfv(o_QT, NH * T).rearrange("p (h t) -> p h t", h=NH)
        o_KT = alloc(NKV * XH // 2)
        KTs = bfv(o_KT, NKV * XH).rearrange("p (h t) -> p h t", h=NKV)
        o_V = alloc(NKT * 512 // 2)
        Vs = bfv(o_V, NKT * 512).rearrange("p (j n) -> p j n", j=NKT)
        att_base = cur[0]
        o_xg = alloc(KC * XH // 2)
        xg = bfv(o_xg, KC * XH).rearrange("p (c t) -> p c t", c=KC)
        o_r1 = alloc(XH)
        rstd1 = f32v(o_r1, XH)
        o_VT = alloc(NKV * XH // 2)
        VTs = bfv(o_VT, NKV * XH).rearrange("p (h t) -> p h t", h=NKV)
        o_xhs = [alloc(XH) for _ in range(2)]
        xhs = [f32v(o, XH) for o in o_xhs]
        o_sqn = [alloc(XH // 2) for _ in range(2)]
        sqn = [bfv(o, XH) for o in o_sqn]
        o_wq = [alloc(KC * 256 // 2) for _ in range(2)]
        wq = [bfv(o, KC * 256).rearrange("p (c n) -> p c n", c=KC) for o in o_wq]
        XT3 = [(0, 494), (494, 988), (988, 1482)]

        def load_wq(n_):
            sl = n_ % 2
            S.add("pool", lambda e: e.dma_start(
                out=wq[sl], in_=w_qkv[:, n_ * 256:(n_ + 1) * 256].rearrange("(c p) n -> p c n", p=128)),
                writes=[("wq", sl)], dma="wq%d" % sl)

        load_wq(0)
        load_wq(1)
        for c in range(KC):
            sl = c % 2
            S.add("sp", lambda e, c=c, sl=sl: e.dma_start(out=xhs[sl], in_=xh[c * 128:(c + 1) * 128, :]),
                  writes=[("xhs", sl)], dma="xhs%d" % sl)
            S.add("act", lambda e, c=c, sl=sl: e.activation(out=xg[:, c, :], in_=xhs[sl], func=AF.Copy,
                                                             scale=g1T[:, c:c + 1]),
                  reads=[("xhs", sl), "par"], writes=[("xg", c)])
            S.add("dve", lambda e, c=c, sl=sl: e.tensor_tensor(out=sqn[sl], in0=xhs[sl], in1=xhs[sl], op=ALU.mult),
                  reads=[("xhs", sl)], writes=[("sqn", sl)])
            for n_, (lo, hi) in enumerate(XT3):
                S.add("pe", lambda e, c=c, sl=sl, n_=n_, lo=lo, hi=hi: e.matmul(
                    ps[:, n_, 0:494], lhsT=onesb, rhs=sqn[sl][:, lo:hi], start=(c == 0), stop=(c == KC - 1)),
                    reads=[("sqn", sl), "onesb"], writes=[("ps", n_)])
        for n_, (lo, hi) in enumerate(XT3):
            rstd_from(rstd1[:, lo:hi], ps[:, n_, 0:494], float(D), [("ps", n_)], [("rstd1", n_)])

        for cc in range(20):
            pair = cc // 2
            sl = pair % 2
            off = (cc % 2) * 128
            if cc < 12:
                tiles = [(QOFF + lo, QOFF + hi) for (lo, hi) in TT]
            else:
                tiles = XT3
            bb = 3 * (cc % 2)
            for n_, (lo, hi) in enumerate(tiles):
                w = hi - lo
                for c in range(KC):
                    S.add("pe", lambda e, c=c, sl=sl, off=off, bb=bb, n_=n_, lo=lo, hi=hi, w=w: e.matmul(
                        ps[:, bb + n_, 0:w], lhsT=wq[sl][:, c, off:off + 128], rhs=xg[:, c, lo:hi],
                        start=(c == 0), stop=(c == KC - 1)),
                        reads=[("wq", sl), ("xg", c)], writes=[("ps", bb + n_)])
            for n_, (lo, hi) in enumerate(tiles):
                w = hi - lo
                if cc < 12:
                    dst = QTs[:, cc, lo - QOFF:hi - QOFF]
                    wk = [("QT", cc, n_)]
                elif cc < 16:
                    dst = KTs[:, cc - 12, lo:hi]
                    wk = [("KT", cc - 12, n_)]
                else:
                    dst = VTs[:, cc - 16, lo:hi]
                    wk = [("VT", cc - 16, n_)]
                rk = [("ps", bb + n_), ("rstd1", 0), ("rstd1", 1), ("rstd1", 2)]
                S.add("dve", lambda e, dst=dst, bb=bb, n_=n_, lo=lo, hi=hi, w=w: e.tensor_tensor(
                    out=dst, in0=ps[:, bb + n_, 0:w], in1=rstd1[:, lo:hi], op=ALU.mult), reads=rk, writes=wk)
            if cc % 2 == 1 and pair + 2 < 10:
                load_wq(pair + 2)

        psb = [ps[:, 6, :].bitcast(BF16), ps[:, 7, :].bitcast(BF16)]
        for j in range(NKT):
            pb = j % 2
            for h in range(NKV):
                S.add("pe", lambda e, j=j, h=h, pb=pb: e.transpose(
                    psb[pb][0:QT_, h * 128:(h + 1) * 128], VTs[:, h, j * QT_:(j + 1) * QT_], identb),
                    reads=[("VT", h, 0), ("VT", h, 1), ("VT", h, 2), "identb"], writes=[("ps", 6 + pb)])
            S.add("act", lambda e, j=j, pb=pb: e.activation(out=Vs[0:QT_, j, :], in_=psb[pb][0:QT_, 0:512], func=AF.Copy),
                  reads=[("ps", 6 + pb)], writes=[("V", j)])

        if _STAGE and _STAGE["name"] == "QKV":
            dump(arena[:, o_QT:o_QT + _STAGE["n"]], _STAGE["n"])

        S.full_barrier()
        cur[0] = att_base
        o_att = alloc(NH * T)
        attT = f32v(o_att, NH * T).rearrange("p (h t) -> p h t", h=NH)
        o_ex = [alloc(171) for _ in range(4)]
        exs = [bfv(o, 342) for o in o_ex]
        o_pt = [alloc(171) for _ in range(4)]
        pts = [bfv(o, 342) for o in o_pt]
        assert cur[0] <= o_E
        o_dn = [alloc(342) for _ in range(2)]
        dns = [f32v(o, 342) for o in o_dn]
        o_sqa = [alloc(T // 2) for _ in range(2)]
        sqa = [bfv(o, T) for o in o_sqa]
        o_ra = alloc(T)
        rstda = f32v(o_ra, T)

        if part in (0, 2):
            for h in range(NH):
                if h not in e_done:
                    e_build(h)
        sc = 128.0 ** -0.5
        LOOK = 3
        steps = [(i, kvh, r) for i in range(NQT) for kvh in range(NKV) for r in range(5)]

        def emit_S(n):
            i, kvh, r = steps[n]
            j = i + r
            sb_ = n % 4
            kc = slice(j * QT_, (j + 1) * QT_)
            qc = slice(i * QT_, (i + 1) * QT_)
            S.add("pe", lambda e: e.matmul(
                ps[0:QT_, sb_, 0:342].rearrange("p (h q) -> p h q", h=3),
                lhsT=KTs[:, kvh, kc], rhs=QTs[:, 3 * kvh:3 * kvh + 3, qc], start=True, stop=True),
                reads=[("KT", kvh, 0), ("KT", kvh, 1), ("KT", kvh, 2)] +
                      [("QT", 3 * kvh + hh, n_) for hh in range(3) for n_ in range(3)],
                writes=[("ps", sb_)])

        def emit_norm(i, kvh, ucnt, qc, ob, db):
            dsl = ucnt % 2
            for hh in range(3):
                h = 3 * kvh + hh
                S.add("act", lambda e, hh=hh, h=h, db=db, dsl=dsl: e.activation(
                    out=dns[dsl][:, hh * QT_:(hh + 1) * QT_], in_=ps[:, db, hh * QT_:(hh + 1) * QT_],
                    func=AF.Ln, bias=es[:, h:h + 1]),
                    reads=[("ps", db), "es"], writes=[("dn", dsl, hh)])
            S.add("act", lambda e, dsl=dsl: e.activation(out=dns[dsl], in_=dns[dsl], func=AF.Exp, scale=-1.0),
                  reads=[("dn", dsl, 0), ("dn", dsl, 1), ("dn", dsl, 2)], writes=[("rc", dsl)])
            S.add("dve", lambda e, kvh=kvh, qc=qc, ob=ob, dsl=dsl: e.tensor_tensor(
                out=attT[:, 3 * kvh:3 * kvh + 3, qc],
                in0=ps[:, ob, 0:342].rearrange("p (h q) -> p h q", h=3),
                in1=dns[dsl].rearrange("p (h q) -> p h q", h=3), op=ALU.mult),
                reads=[("ps", ob), ("rc", dsl)], writes=[("att", 3 * kvh + hh, i) for hh in range(3)])

        pending_norm = []
        for n in range(min(LOOK, len(steps))):
            emit_S(n)
        for n, (i, kvh, r) in enumerate(steps):
            if n + LOOK < len(steps):
                emit_S(n + LOOK)
            ucnt = n // 5
            qc = slice(i * QT_, (i + 1) * QT_)
            ob = 4 + (ucnt % 2)
            db = 6 + (ucnt % 2)
            j = i + r
            sb_ = n % 4
            es_ = n % 4
            S.add("act", lambda e, sb_=sb_, es_=es_: e.activation(
                out=exs[es_][0:QT_, :], in_=ps[0:QT_, sb_, 0:342], func=AF.Exp, scale=sc),
                reads=[("ps", sb_)], writes=[("ex", es_)])
            S.add("dve", lambda e, es_=es_, kvh=kvh, r=r: e.tensor_tensor(
                out=pts[es_][0:QT_, :].rearrange("p (h q) -> p h q", h=3),
                in0=exs[es_][0:QT_, :].rearrange("p (h q) -> p h q", h=3),
                in1=Eb[0:QT_, 3 * kvh:3 * kvh + 3, r, :], op=ALU.mult),
                reads=[("ex", es_)] + [("E", 3 * kvh + hh) for hh in range(3)], writes=[("pt", es_)])
            S.add("pe", lambda e, j=j, kvh=kvh, es_=es_, ob=ob, r=r: e.matmul(
                ps[:, ob, 0:342], lhsT=Vs[0:QT_, j, kvh * 128:(kvh + 1) * 128], rhs=pts[es_][0:QT_, :],
                start=(r == 0), stop=(r == 4)),
                reads=[("V", j), ("pt", es_)], writes=[("ps", ob)])
            S.add("pe", lambda e, j=j, es_=es_, db=db, r=r: e.matmul(
                ps[:, db, 0:342], lhsT=vones[0:QT_, j, :], rhs=pts[es_][0:QT_, :],
                start=(r == 0), stop=(r == 4)),
                reads=[("vones", j), ("pt", es_)], writes=[("ps", db)])
            for rel_n, fn_ in list(pending_norm):
                if rel_n <= n:
                    fn_()
                    pending_norm.remove((rel_n, fn_))
            if r == 4:
                pending_norm.append((n + 2, (lambda i=i, kvh=kvh, ucnt=ucnt, qc=qc, ob=ob, db=db: emit_norm(i, kvh, ucnt, qc, ob, db))))
        for rel_n, fn_ in pending_norm:
            fn_()

        for h in range(NH):
            sl = h % 2
            S.add("dve", lambda e, h=h, sl=sl: e.tensor_tensor(out=sqa[sl], in0=attT[:, h, :], in1=attT[:, h, :], op=ALU.mult),
                  reads=[("att", h, i) for i in range(NQT)], writes=[("sqa", sl)])
            for n_, (lo, hi) in enumerate(TT):
                S.add("pe", lambda e, h=h, sl=sl, n_=n_, lo=lo, hi=hi: e.matmul(
                    ps[:, n_, 0:342], lhsT=onesb, rhs=sqa[sl][:, lo:hi], start=(h == 0), stop=(h == NH - 1)),
                    reads=[("sqa", sl), "onesb"], writes=[("ps", n_)])
        for n_, (lo, hi) in enumerate(TT):
            rstd_from(rstda[:, lo:hi], ps[:, n_, 0:342], 1536.0, [("ps", n_)], [("rstda", n_)])
        for h in range(NH):
            S.add("dve", lambda e, h=h: e.scalar_tensor_tensor(
                out=mixT[:, h, :], in0=attT[:, h, :], scalar=gaT[:, h:h + 1], in1=rstda,
                op0=ALU.mult, op1=ALU.mult),
                reads=[("att", h, i) for i in range(NQT)] + [("rstda", n_) for n_ in range(3)] + ["par"],
                writes=[("mix", h, p) for p in range(3)])

        if _STAGE and _STAGE["name"] == "ATT":
            dump(arena[:, o_mix:o_mix + _STAGE["n"]], _STAGE["n"])

        S.full_barrier()
        cur[0] = base
        o_x1 = alloc(KC * T)
        x1T = f32v(o_x1, KC * T).rearrange("p (c t) -> p c t", c=KC)
        o_h2 = alloc(KC * T // 2)
        h2T = bfv(o_h2, KC * T).rearrange("p (c t) -> p c t", c=KC)
        ffn_base = cur[0]
        o_wo = [alloc(KC * 256 // 2) for _ in range(2)]
        wo = [bfv(o, KC * 256).rearrange("p (c n) -> p c n", c=KC) for o in o_wo]
        o_xr = [alloc(T) for _ in range(2)]
        xr = [f32v(o, T) for o in o_xr]
        o_sq2 = [alloc(T // 2) for _ in range(2)]
        sq2 = [bfv(o, T) for o in o_sq2]
        o_r2 = alloc(T)
        rstd2 = f32v(o_r2, T)

        def load_wo(n_):
            sl = n_ % 2
            S.add("pool", lambda e: e.dma_start(
                out=wo[sl], in_=w_out[:, n_ * 256:(n_ + 1) * 256].rearrange("(c p) n -> p c n", p=128)),
                writes=[("wo", sl)], dma="wo%d" % sl)

        load_wo(0)
        load_wo(1)
        for m in range(KC):
            pair = m // 2
            sl = pair % 2
            off = (m % 2) * 128
            bb = 3 * (m % 2)
            xs = m % 2
            S.add("sp", lambda e, m=m, xs=xs: e.dma_start(out=xr[xs], in_=xh[m * 128:(m + 1) * 128, QOFF:QOFF + T]),
                  writes=[("xr", xs)], dma="xr%d" % xs)
            for n_, (lo, hi) in enumerate(TT):
                for k in range(KC):
                    S.add("pe", lambda e, k=k, sl=sl, off=off, bb=bb, n_=n_, lo=lo, hi=hi: e.matmul(
                        ps[:, bb + n_, 0:342], lhsT=wo[sl][:, k, off:off + 128], rhs=mixT[:, k, lo:hi],
                        start=(k == 0), stop=(k == KC - 1)),
                        reads=[("wo", sl), ("mix", k, n_)], writes=[("ps", bb + n_)])
            for n_, (lo, hi) in enumerate(TT):
                S.add("dve", lambda e, m=m, bb=bb, n_=n_, lo=lo, hi=hi, xs=xs: e.tensor_tensor(
                    out=x1T[:, m, lo:hi], in0=ps[:, bb + n_, 0:342], in1=xr[xs][:, lo:hi], op=ALU.add),
                    reads=[("ps", bb + n_), ("xr", xs)], writes=[("x1", m, n_)])
            if m % 2 == 1 and pair + 2 < 8:
                load_wo(pair + 2)

        def rms_stats(src_fn, ncols, tiles, sqbuf, key_src_fn, outr, outkey, nfeat):
            for c in range(KC):
                sl = c % 2
                S.add("dve", lambda e, c=c, sl=sl: e.tensor_tensor(out=sqbuf[sl][:, 0:ncols], in0=src_fn(c), in1=src_fn(c), op=ALU.mult),
                      reads=key_src_fn(c), writes=[("sqb", sl)])
                for n_, (lo, hi) in enumerate(tiles):
                    S.add("pe", lambda e, c=c, sl=sl, n_=n_, lo=lo, hi=hi: e.matmul(
                        ps[:, n_, 0:hi - lo], lhsT=onesb, rhs=sqbuf[sl][:, lo:hi], start=(c == 0), stop=(c == KC - 1)),
                        reads=[("sqb", sl), "onesb"], writes=[("ps", n_)])
            for n_, (lo, hi) in enumerate(tiles):
                rstd_from(outr[:, lo:hi], ps[:, n_, 0:hi - lo], nfeat, [("ps", n_)], [(outkey, n_)])

        rms_stats(lambda c: x1T[:, c, :], T, TT, sq2, lambda c: [("x1", c, n_) for n_ in range(3)], rstd2, "rstd2", float(D))
        for c in range(KC):
            S.add("dve", lambda e, c=c: e.scalar_tensor_tensor(
                out=h2T[:, c, :], in0=x1T[:, c, :], scalar=g2T[:, c:c + 1], in1=rstd2, op0=ALU.mult, op1=ALU.mult),
                reads=[("x1", c, n_) for n_ in range(3)] + [("rstd2", n_) for n_ in range(3)] + ["par"],
                writes=[("h2", c)])
        S.add("dve", lambda e: e.tensor_scalar(out=h2T[:, :, 0:1], in0=h2T[:, :, 0:1], scalar1=qv[:, 0:1], scalar2=None,
                                               op0=ALU.mult),
              reads=[("h2", c) for c in range(KC)] + ["par"], writes=[("h2", c) for c in range(KC)])
        S.add("dve", lambda e: e.tensor_scalar(out=h2T[:, :, T - 1:T], in0=h2T[:, :, T - 1:T], scalar1=qv[:, 1:2],
                                               scalar2=None, op0=ALU.mult),
              reads=[("h2", c) for c in range(KC)] + ["par"], writes=[("h2", c) for c in range(KC)])

        if _STAGE and _STAGE["name"] == "X1":
            dump(arena[:, o_x1:o_x1 + _STAGE["n"]], _STAGE["n"])

        S.full_barrier()
        cur[0] = ffn_base
        GMAX = max(FGROUPS)
        o_wg = [alloc(KC * 256 // 2) for _ in range(2)]
        wg = [bfv(o, KC * 256).rearrange("p (c n) -> p c n", c=KC) for o in o_wg]
        o_wv = [alloc(KC * 256 // 2) for _ in range(2)]
        wv = [bfv(o, KC * 256).rearrange("p (c n) -> p c n", c=KC) for o in o_wv]
        o_act = [alloc(GMAX * TOK // 2) for _ in range(2)]
        actb = [bfv(o, GMAX * TOK).rearrange("p (f t) -> p f t", f=GMAX) for o in o_act]
        _save = cur[0]
        cur[0] = o_mix
        o_wd = [alloc(GMAX * 512 // 2) for _ in range(2)]
        wd = [bfv(o, GMAX * 512).rearrange("p (f n) -> p f n", f=GMAX) for o in o_wd]
        o_t1 = [alloc(342) for _ in range(3)]
        t1s = [f32v(o, 342) for o in o_t1]
        o_ge = [alloc(342) for _ in range(3)]
        ges = [f32v(o, 342) for o in o_ge]
        assert cur[0] <= o_mix + KC * T // 2
        cur[0] = _save
        GT = [(0, 344), (342, 686), (684, 1026)]
        OT = [(1, 343), (343, 685), (685, 1025)]

        npairs = (NFF + 1) // 2
        f_lo = {2: 0, 3: FSPLIT}.get(part, 0)
        pair_hi = {0: npairs, 2: FSPLIT // 2, 3: npairs}.get(part, 0)

        def load_wup(pi):
            if pi >= pair_hi:
                return
            sl = pi % 2
            f0 = pi * 2 - f_lo
            nf = min(2, NFF - pi * 2)
            S.add("pool", lambda e: e.dma_start(
                out=wg[sl][:, :, 0:nf * 128],
                in_=w_upg[:, f0 * 128:(f0 + nf) * 128].rearrange("(c p) n -> p c n", p=128)),
                writes=[("wg", sl)], dma="wg%d" % sl)
            S.add("pool", lambda e: e.dma_start(
                out=wv[sl][:, :, 0:nf * 128],
                in_=w_upv[:, f0 * 128:(f0 + nf) * 128].rearrange("(c p) n -> p c n", p=128)),
                writes=[("wv", sl)], dma="wv%d" % sl)

        nwd = 0

        def load_wd(f0, gsz, q):
            nonlocal nwd
            sl = nwd % 2
            nwd += 1
            S.add("pool", lambda e: e.dma_start(
                out=wd[sl][:, 0:gsz, :],
                in_=w_dn[(f0 - f_lo) * 128:(f0 - f_lo + gsz) * 128, q * 512:(q + 1) * 512].rearrange("(f p) n -> p f n", p=128)),
                writes=[("wd", sl)], dma="wd%d" % sl)
            return sl

        f = 0
        dcnt = 0
        assert sum(FGROUPS[:2]) == FSPLIT and FSPLIT % 2 == 0
        for gi, gsz in enumerate(FGROUPS):
            if gi == 0:
                S = S_real if part in (0, 2) else S_null
                load_wup(0)
                load_wup(1)
            if gi == 2:
                if part == 2:
                    S.full_barrier()
                    S.add("sp", lambda e: e.dma_start(out=x1_o, in_=arena[:, o_x1:o_x1 + KC * T]), dma="x1o")
                    S.add("sp", lambda e: e.dma_start(out=h2_o, in_=arena[:, o_h2:o_h2 + KC * T // 2].bitcast(BF16)), dma="h2o")
                    S.full_barrier()
                    S.add("sp", lambda e: None)
                S = S_real if part in (0, 3) else S_null
                if part == 3:
                    S.add("sp", lambda e: e.dma_start(out=arena[:, o_x1:o_x1 + KC * T], in_=x1_i),
                          writes=[("x2", m, n_) for m in range(KC) for n_ in range(2)], dma="x1i")
                    S.add("sp", lambda e: e.dma_start(out=arena[:, o_h2:o_h2 + KC * T // 2].bitcast(BF16), in_=h2_i),
                          writes=[("h2", c) for c in range(KC)], dma="h2i")
                    S.full_barrier()
                    load_wup(FSPLIT // 2)
                    load_wup(FSPLIT // 2 + 1)
            asl = gi % 2
            f0g = f
            wd_pre = (load_wd(f0g, gsz, 0), load_wd(f0g, gsz, 1))
            for fi in range(gsz):
                pi = f // 2
                sl = pi % 2
                off = (f % 2) * 128
                for n_, (lo, hi) in enumerate(GT):
                    for k in range(KC):
                        S.add("pe", lambda e, k=k, sl=sl, off=off, n_=n_, lo=lo, hi=hi: e.matmul(
                            ps[:, n_, 0:hi - lo], lhsT=wg[sl][:, k, off:off + 128], rhs=h2T[:, k, lo:hi],
                            start=(k == 0), stop=(k == KC - 1)),
                            reads=[("wg", sl), ("h2", k)], writes=[("ps", n_)])
                for n_, (lo, hi) in enumerate(OT):
                    for k in range(KC):
                        S.add("pe", lambda e, k=k, sl=sl, off=off, n_=n_, lo=lo, hi=hi: e.matmul(
                            ps[:, 3 + n_, 0:hi - lo], lhsT=wv[sl][:, k, off:off + 128], rhs=h2T[:, k, lo:hi],
                            start=(k == 0), stop=(k == KC - 1)),
                            reads=[("wv", sl), ("h2", k)], writes=[("ps", 3 + n_)])
                if f % 2 == 1 and pi + 2 < npairs:
                    load_wup(pi + 2)
                for n_, (lo, hi) in enumerate(OT):
                    w = hi - lo
                    S.add("act", lambda e, f=f, n_=n_, w=w: e.activation(
                        out=t1s[n_][:, 0:w], in_=ps[:, n_, 1:1 + w], func=AF.Identity,
                        scale=dwT[:, f, 1:2], bias=dwT[:, f, 3:4]),
                        reads=[("ps", n_), "par"], writes=[("t1", n_)])
                for n_, (lo, hi) in enumerate(OT):
                    w = hi - lo
                    S.add("dve", lambda e, f=f, n_=n_, w=w: e.scalar_tensor_tensor(
                        out=t1s[n_][:, 0:w], in0=ps[:, n_, 0:w], scalar=dwT[:, f, 0:1], in1=t1s[n_][:, 0:w],
                        op0=ALU.mult, op1=ALU.add),
                        reads=[("ps", n_), ("t1", n_), "par"], writes=[("t1", n_)])
                for n_, (lo, hi) in enumerate(OT):
                    w = hi - lo
                    S.add("dve", lambda e, f=f, n_=n_, w=w: e.scalar_tensor_tensor(
                        out=t1s[n_][:, 0:w], in0=ps[:, n_, 2:2 + w], scalar=dwT[:, f, 2:3], in1=t1s[n_][:, 0:w],
                        op0=ALU.mult, op1=ALU.add),
                        reads=[("ps", n_), ("t1", n_), "par"], writes=[("t1", n_)])
                for n_, (lo, hi) in enumerate(OT):
                    w = hi - lo
                    S.add("act", lambda e, n_=n_, w=w: e.activation(out=ges[n_][:, 0:w], in_=t1s[n_][:, 0:w], func=AF.Gelu),
                          reads=[("t1", n_)], writes=[("ge", n_)])
                for n_, (lo, hi) in enumerate(OT):
                    w = hi - lo
                    S.add("dve", lambda e, n_=n_, w=w, fi=fi, lo=lo, hi=hi, asl=asl: e.tensor_tensor(
                        out=actb[asl][:, fi, lo - 1:hi - 1], in0=ges[n_][:, 0:w], in1=ps[:, 3 + n_, 0:w], op=ALU.mult),
                        reads=[("ge", n_), ("ps", 3 + n_)], writes=[("act", asl, fi)])
                f += 1
            wsl, wsl1 = wd_pre
            for q in range(4):
                if q == 0:
                    nxt = wsl1
                else:
                    nxt = load_wd(f0g, gsz, q + 1) if q + 1 < 4 else None
                for mm in range(4):
                    m = q * 4 + mm
                    for n_ in range(2):
                        bk = 6 + (dcnt % 2)
                        dcnt += 1
                        for fi in range(gsz):
                            S.add("pe", lambda e, fi=fi, wsl=wsl, mm=mm, n_=n_, bk=bk, asl=asl, gsz=gsz: e.matmul(
                                ps[:, bk, 0:512], lhsT=wd[wsl][:, fi, mm * 128:(mm + 1) * 128],
                                rhs=actb[asl][:, fi, n_ * 512:(n_ + 1) * 512], start=(fi == 0), stop=(fi == gsz - 1)),
                                reads=[("wd", wsl), ("act", asl, fi)], writes=[("ps", bk)])
                        S.add("dve", lambda e, m=m, n_=n_, bk=bk: e.tensor_tensor(
                            out=x1T[:, m, 1 + n_ * 512:1 + (n_ + 1) * 512], in0=ps[:, bk, 0:512],
                            in1=x1T[:, m, 1 + n_ * 512:1 + (n_ + 1) * 512], op=ALU.add),
                            reads=[("ps", bk), ("x2", m, n_)], writes=[("x2", m, n_)])
                wsl = nxt

        if _STAGE and _STAGE["name"] == "X2":
            dump(arena[:, o_x1:o_x1 + _STAGE["n"]], _STAGE["n"])

        S.full_barrier()
        cur[0] = ffn_base
        o_sq3 = [alloc(TOK // 2) for _ in range(2)]
        sq3 = [bfv(o, TOK) for o in o_sq3]
        o_r3 = alloc(TOK)
        rstd3 = f32v(o_r3, TOK)
        o_yo = [alloc(TOK) for _ in range(2)]
        yo = [f32v(o, TOK) for o in o_yo]
        T2 = [(0, 512), (512, 1024)]
        rms_stats(lambda c: x1T[:, c, 1:1 + TOK], TOK, T2, sq3,
                  lambda c: [("x2", c, 0), ("x2", c, 1)], rstd3, "rstd3", float(D))
        outs = []
        for c in range(KC):
            sl = c % 2
            S.add("dve", lambda e, c=c, sl=sl: e.scalar_tensor_tensor(
                out=yo[sl], in0=x1T[:, c, 1:1 + TOK], scalar=gnT[:, c:c + 1], in1=rstd3, op0=ALU.mult, op1=ALU.mult),
                reads=[("x2", c, 0), ("x2", c, 1), ("rstd3", 0), ("rstd3", 1), "par"], writes=[("yo", sl)])
            outs.append(S.add("sp", lambda e, c=c, sl=sl: e.dma_start(out=yT[c * 128:(c + 1) * 128, :], in_=yo[sl]),
                              reads=[("yo", sl)], dma="yo%d" % sl))
        S.full_barrier()
        S.add("sp", lambda e: None)

        def sem_alloc(name):
            return sem_pool.pop()

        S_real.emit(nc, block, sem_alloc)
    return nc


def _alibi_slopes(n_heads):
    def pow2_slopes(n):
        start = 2.0 ** (-8.0 / n)
        return [start ** (i + 1) for i in range(n)]
    if math.log2(n_heads).is_integer():
        s = pow2_slopes(n_heads)
    else:
        closest = 2 ** int(math.floor(math.log2(n_heads)))
        s = pow2_slopes(closest) + pow2_slopes(2 * closest)[0::2][: n_heads - closest]
    return np.array(s, dtype=np.float32)


_CONST_CACHE = {}


def _constants():
    if _CONST_CACHE:
        return _CONST_CACHE
    bf = ml_dtypes.bfloat16
    slopes = _alibi_slopes(NH)
    kk = np.arange(QT_)[:, None, None]
    r = np.arange(5)[None, :, None]
    qq = np.arange(QT_)[None, None, :]
    rel = QT_ * (r - 2) + kk - qq
    ab = np.empty((QT_, NH, 5, QT_), np.float32)
    for h in range(NH):
        ab[:, h] = np.where(np.abs(rel) <= 128, -slopes[h] * np.abs(rel).astype(np.float32), -30000.0)
    _CONST_CACHE["abias"] = ab.reshape(QT_, NH * 5 * QT_)
    k = np.arange(128)
    ang = 2.0 * np.pi * ((k[:, None] * k[None, :]) % 128) / 128.0
    cc = np.concatenate([np.cos(ang) / 1024.0, np.sin(ang) / 1024.0], axis=1)
    _CONST_CACHE["cc128"] = cc.astype(np.float32).astype(bf)
    s = np.arange(SEQ, dtype=np.int64)[:, None]
    tabs = []
    for c in range(NCORES):
        t = (np.arange(T, dtype=np.int64) + 1024 * c - 1) % SEQ
        ang = 2.0 * np.pi * ((s * t[None, :]) % SEQ).astype(np.float64) / SEQ
        co = np.cos(ang).astype(np.float32).astype(bf)
        si = np.sin(ang).astype(np.float32).astype(bf)
        tb = np.empty((3, SEQ, 2 * 342), bf)
        for p, (lo, hi) in enumerate(TT):
            tb[p, :, 0:342] = co[:, lo:hi]
            tb[p, :, 342:684] = si[:, lo:hi]
        tabs.append(tb)
    _CONST_CACHE["tab"] = tabs
    cst = np.zeros((128, 256), np.float32)
    cst[:, 0:128] = np.eye(128, dtype=np.float32)
    _CONST_CACHE["cst"] = cst
    return _CONST_CACHE


def _colmajor(v, n):
    return np.ascontiguousarray(np.asarray(v, np.float32).reshape(n, 128).T)


def _prep(inputs):
    C = _constants()
    x = np.asarray(inputs["x"], np.float32)[0]
    xT = np.ascontiguousarray(x.T)
    w_in = np.asarray(inputs["w_in"], np.float32)
    w_up = np.asarray(inputs["w_up"], np.float32)
    w_down = np.asarray(inputs["w_down"], np.float32)
    par0 = np.zeros((128, 320), np.float32)
    par0[:, 0:16] = _colmajor(inputs["norm1_g"], 16)
    par0[:, 16:32] = _colmajor(inputs["norm2_g"], 16)
    par0[:, 32:48] = _colmajor(inputs["normf_g"], 16)
    par0[:, 48:60] = _colmajor(inputs["attn_out_g"], 12)
    par0[:, 60:64] = _colmajor(inputs["fourier_out_g"], 4)
    par0[:, 64:76] = np.asarray(inputs["sink"], np.float32)[None, :]
    dw = np.asarray(inputs["dw_w"], np.float32)[:, 0, :]
    db = np.asarray(inputs["dw_b"], np.float32)
    dwp = np.zeros((128, 43, 4), np.float32)
    for k in range(3):
        dwp[:, :, k] = _colmajor(dw[k], 43)
    dwp[:, :, 3] = _colmajor(db, 43)
    par0[:, 96:96 + 172] = dwp.reshape(128, 172)
    fs = FSPLIT * 128
    sh = {
        "xT": xT,
        "w_u": np.ascontiguousarray(w_in[:, 2560:3072]),
        "w_qkv": np.ascontiguousarray(w_in[:, 0:2560]),
        "w_out": np.ascontiguousarray(np.asarray(inputs["w_out"], np.float32)),
        "w_fou": np.ascontiguousarray(np.asarray(inputs["w_fourier"], np.float32)),
        "upg2": np.ascontiguousarray(w_up[:, 0:fs]), "upv2": np.ascontiguousarray(w_up[:, DFF:DFF + fs]),
        "dn2": np.ascontiguousarray(w_down[0:fs, :]),
        "upg3": np.ascontiguousarray(w_up[:, fs:DFF]), "upv3": np.ascontiguousarray(w_up[:, DFF + fs:2 * DFF]),
        "dn3": np.ascontiguousarray(w_down[fs:, :]),
        "upg": np.ascontiguousarray(w_up[:, 0:DFF]), "upv": np.ascontiguousarray(w_up[:, DFF:2 * DFF]),
        "dn": np.ascontiguousarray(w_down),
    }
    pars, xhs = [], []
    for c in range(NCORES):
        t0 = 1024 * c - 1
        tok = t0 - QOFF + np.arange(XH)
        valid = (tok >= 0) & (tok < SEQ)
        xhc = np.zeros((D, XH), np.float32)
        xhc[:, valid] = xT[:, tok[valid]]
        par = par0.copy()
        par[0:QT_, 76:89] = valid.astype(np.float32).reshape(NKT, QT_).T
        par[:, 89] = 1.0 if t0 >= 0 else 0.0
        par[:, 90] = 1.0 if t0 + T - 1 < SEQ else 0.0
        pars.append(par)
        xhs.append(xhc)
    return C, sh, pars, xhs


def _maps1(C, sh, pars, xhs, c):
    return {"xT": sh["xT"], "w_u": sh["w_u"], "w_fou": sh["w_fou"], "tab": C["tab"][c], "cc128": C["cc128"],
            "par": pars[c], "cst": C["cst"]}


def _maps2(C, sh, pars, xhs, c, fT):
    return {"xh": xhs[c], "w_qkv": sh["w_qkv"], "w_out": sh["w_out"], "w_upg": sh["upg2"], "w_upv": sh["upv2"],
            "w_dn": sh["dn2"], "abias": C["abias"], "par": pars[c], "cst": C["cst"], "fT": fT}


def _maps3(C, sh, pars, xhs, c, x1, h2):
    return {"w_upg": sh["upg3"], "w_upv": sh["upv3"], "w_dn": sh["dn3"], "par": pars[c], "cst": C["cst"],
            "x1i": x1, "h2i": h2}


_NC_CACHE = {}


def _prog(part):
    if part not in _NC_CACHE:
        _NC_CACHE[part] = build_program(part)
    return _NC_CACHE[part]


FUSED = True


def _maps0(C, sh, pars, xhs, c):
    return {"xT": sh["xT"], "w_u": sh["w_u"], "w_fou": sh["w_fou"], "tab": C["tab"][c], "cc128": C["cc128"],
            "par": pars[c], "cst": C["cst"], "xh": xhs[c], "w_qkv": sh["w_qkv"], "w_out": sh["w_out"],
            "w_upg": sh["upg"], "w_upv": sh["upv"], "w_dn": sh["dn"], "abias": C["abias"]}


def kernel(**inputs):
    C, sh, pars, xhs = _prep(inputs)
    cores = list(range(NCORES))
    if FUSED:
        r = run_bass_kernel_spmd(_prog(0), [_maps0(C, sh, pars, xhs, c) for c in cores], core_ids=cores).results
        out = np.empty((1, SEQ, D), np.float32)
        for c in cores:
            out[0, c * TOK:(c + 1) * TOK, :] = np.asarray(r[c]["yT"], np.float32).T
        return out
    r1 = run_bass_kernel_spmd(_prog(1), [_maps1(C, sh, pars, xhs, c) for c in cores], core_ids=cores).results
    r2 = run_bass_kernel_spmd(_prog(2), [_maps2(C, sh, pars, xhs, c, np.asarray(r1[c]["fT"])) for c in cores],
                              core_ids=cores).results
    r3 = run_bass_kernel_spmd(_prog(3), [_maps3(C, sh, pars, xhs, c, np.asarray(r2[c]["x1o"]), np.asarray(r2[c]["h2o"]))
                                         for c in cores], core_ids=cores).results
    out = np.empty((1, SEQ, D), np.float32)
    for c in cores:
        out[0, c * TOK:(c + 1) * TOK, :] = np.asarray(r3[c]["yT"], np.float32).T
    return out
```

```python
import math
from contextlib import ExitStack
import numpy as np
import ml_dtypes
import concourse.bass as bass
import concourse.mybir as mybir
from concourse.bass_utils import run_bass_kernel_spmd

F32 = mybir.dt.float32
BF16 = mybir.dt.bfloat16
ALU = mybir.AluOpType
AF = mybir.ActivationFunctionType
AX = mybir.AxisListType

NCORES = 8
D = 2048
KC = 16
SEQ = 8192
TOK = 1024
T = 1026
TT = [(0, 342), (342, 684), (684, 1026)]
QT_ = 114
NQT = 9
NKT = 13
XH = NKT * QT_
QOFF = 2 * QT_
NH = 12
NKV = 4
DFF = 5504
NFF = 43
EPS = 1e-6
FGROUPS = [9, 9, 9, 8, 8]
ARENA = 52480

_STAGE = None
_DBG = {}


class Op:
    __slots__ = ("eng", "fn", "deps", "dma", "sig", "val", "sem", "idx")

    def __init__(self, eng, fn, deps, dma):
        self.eng, self.fn, self.deps, self.dma = eng, fn, deps, dma
        self.sig = False
        self.val = None
        self.sem = None


class Sched:
    ENGS = ("pe", "act", "dve", "pool", "sp")

    def __init__(self):
        self.ops = {e: [] for e in self.ENGS}
        self.lastw = {}
        self.readers = {}
        self.barrier = {}
        self.nops = 0

    def add(self, eng, fn, reads=(), writes=(), dma=None, extra=()):
        deps = []
        for k in reads:
            w = self.lastw.get(k)
            if w is not None:
                deps.append(w)
        for k in writes:
            w = self.lastw.get(k)
            if w is not None and (w.eng != eng or eng != "pe" or w.dma is not None or dma is not None):
                deps.append(w)
            for r in self.readers.get(k, {}).values():
                if r.eng != eng or eng != "pe" or r.dma is not None or dma is not None:
                    deps.append(r)
        deps.extend(self.barrier.values())
        deps.extend(extra)
        seen = set()
        dd = []
        for d in deps:
            if id(d) not in seen:
                seen.add(id(d))
                dd.append(d)
        op = Op(eng, fn, dd, dma)
        op.idx = self.nops
        self.nops += 1
        for d in dd:
            d.sig = True
        self.ops[eng].append(op)
        for k in reads:
            self.readers.setdefault(k, {})[(eng, dma)] = op
        for k in writes:
            self.lastw[k] = op
            self.readers[k] = {}
        return op

    def full_barrier(self):
        b = {}
        for e in self.ENGS:
            if self.ops[e]:
                b[e] = self.ops[e][-1]
        lastdma = {}
        for e in self.ENGS:
            for op in self.ops[e]:
                if op.dma is not None:
                    lastdma[op.dma] = op
        for ch, op in lastdma.items():
            b[("dma", ch)] = op
        self.barrier = b

    def emit(self, nc, block, sem_alloc):
        eng_sem = {e: sem_alloc("s_" + e) for e in self.ENGS}
        chan_sem = {}
        chan_cnt = {}
        for e in self.ENGS:
            cnt = 0
            for op in self.ops[e]:
                if op.dma is not None:
                    if op.dma not in chan_sem:
                        chan_sem[op.dma] = sem_alloc("d_" + op.dma)
                        chan_cnt[op.dma] = 0
                    chan_cnt[op.dma] += 16
                    op.val = chan_cnt[op.dma]
                    op.sem = chan_sem[op.dma]
                elif op.sig:
                    cnt += 1
                    op.val = cnt
                    op.sem = eng_sem[e]

        def runner(ename):
            def run(e):
                waited = {}
                for op in self.ops[ename]:
                    for d in op.deps:
                        key = id(d.sem)
                        if waited.get(key, 0) < d.val:
                            e.wait_ge(d.sem, d.val)
                            waited[key] = d.val
                    ins = op.fn(e)
                    if ins is None:
                        if not op.sig:
                            continue
                        ins = e.nop()
                    if op.dma is not None:
                        ins.then_inc(op.sem, 16)
                    elif op.sig:
                        ins.then_inc(op.sem, 1)
            return run

        block.tensor(runner("pe"))
        block.scalar(runner("act"))
        block.vector(runner("dve"))
        block.gpsimd(runner("pool"))
        block.sync(runner("sp"))


FSPLIT = 18


class LimitSched:
    def __init__(self, real, n):
        self.real, self.n = real, n

    def add(self, *a, **k):
        if self.n <= 0:
            return None
        self.n -= 1
        return self.real.add(*a, **k)

    def full_barrier(self):
        self.real.full_barrier()


class NullSched:
    def add(self, *a, **k):
        return None

    def full_barrier(self):
        pass


def build_program(part):
    nc = bass.Bass("TRN2", target_bir_lowering=False)

    def din(name, shape, dt=F32, parts=(1, 2, 3)):
        if part != 0 and part not in parts:
            return None
        return nc.dram_tensor(name, list(shape), dt, kind="ExternalInput").ap()

    def dout(name, shape, dt, parts):
        if part not in parts and not (part == 0 and name == "yT"):
            return None
        return nc.dram_tensor(name, list(shape), dt, kind="ExternalOutput").ap()

    NB = _DBG.get("nb", 16)
    NQ = _DBG.get("nq", 16)
    xT = din("xT", [D, 512 * NB], parts=(1,))
    xh = din("xh", [D, XH], parts=(2,))
    w_u = din("w_u", [D, 512], parts=(1,))
    w_qkv = din("w_qkv", [D, 2560], parts=(2,))
    w_out = din("w_out", [D, D], parts=(2,))
    nf_part = {0: NFF, 2: FSPLIT, 3: NFF - FSPLIT}.get(part, 1)
    w_upg = din("w_upg", [D, nf_part * 128], parts=(2, 3))
    w_upv = din("w_upv", [D, nf_part * 128], parts=(2, 3))
    w_dn = din("w_dn", [nf_part * 128, D], parts=(2, 3))
    w_fou = din("w_fou", [4, 128, 128], parts=(1,))
    tab = din("tab", [3, 512 * NQ, 2 * 342], BF16, parts=(1,))
    cc128 = din("cc128", [128, 2 * 128], BF16, parts=(1,))
    abias = din("abias", [QT_, NH * 5 * QT_], parts=(2,))
    par = din("par", [128, 320])
    cst = din("cst", [128, 256])
    fT_o = dout("fT", [128, 4 * T], BF16, (1,))
    fT_i = din("fT", [128, 4 * T], BF16, parts=(2,)) if part != 0 else None
    x1_o = dout("x1o", [128, KC * T], F32, (2,))
    h2_o = dout("h2o", [128, KC * T], BF16, (2,))
    x1_i = din("x1i", [128, KC * T], F32, parts=(3,)) if part != 0 else None
    h2_i = din("h2i", [128, KC * T], BF16, parts=(3,)) if part != 0 else None
    yT = dout("yT", [D, TOK], F32, (3,))
    dbg = None
    if _STAGE is not None:
        dbg = nc.dram_tensor("dbg", [128, _STAGE["n"]], F32, kind="ExternalOutput").ap()

    S_real = Sched()
    S_null = NullSched()
    S = S_real

    with ExitStack() as stack:
        arena = stack.enter_context(nc.sbuf_tensor("arena", [128, ARENA], F32))
        ps = stack.enter_context(nc.psum_tensor("ps", [128, 8, 512], F32))
        sem_pool = [stack.enter_context(nc.semaphore("sm%d" % i)) for i in range(56)]
        block = stack.enter_context(nc.Block())
        def f32v(off, n):
            return arena[:, off:off + n]

        def bfv(off, n_bf):
            assert n_bf % 2 == 0
            return arena[:, off:off + n_bf // 2].bitcast(BF16)

        cur = [0]

        def alloc(nwords):
            o = cur[0]
            cur[0] += nwords
            assert cur[0] <= ARENA, (cur[0], ARENA)
            return o

        o_par = alloc(320)
        P_ = f32v(o_par, 320)
        g1T = P_[:, 0:16]
        g2T = P_[:, 16:32]
        gnT = P_[:, 32:48]
        gaT = P_[:, 48:60]
        gfoT = P_[:, 60:64]
        sinkB = P_[:, 64:76]
        kvalid = P_[:, 76:89]
        qv = P_[:, 89:91]
        dwT = P_[:, 96:96 + 43 * 4].rearrange("p (f k) -> p f k", k=4)
        o_es = alloc(12)
        es = f32v(o_es, 12)
        o_eps = alloc(2)
        epsT = f32v(o_eps, 1)
        o_id = alloc(128)
        identf = f32v(o_id, 128)
        o_idb = alloc(64)
        identb = bfv(o_idb, 128)
        o_on = alloc(64)
        onesb = bfv(o_on, 128)
        o_vo = alloc(NKT * 64)
        vones = bfv(o_vo, NKT * 128).rearrange("p (j m) -> p j m", j=NKT)
        o_mix = alloc(KC * T // 2)
        mixT = bfv(o_mix, KC * T).rearrange("p (c t) -> p c t", c=KC)
        base = cur[0]

        ld_par = S.add("sp", lambda e: e.dma_start(out=P_, in_=par), writes=["par"], dma="par")
        ld_id = S.add("sp", lambda e: e.dma_start(out=identf, in_=cst[:, 0:128]), writes=["identf"], dma="cst")
        S.add("dve", lambda e: e.tensor_copy(out=identb, in_=identf), reads=["identf"], writes=["identb"])
        S.add("dve", lambda e: e.memset(onesb, 1.0), writes=["onesb"])
        S.add("dve", lambda e: e.memset(epsT, EPS), writes=["epsT"])
        S.add("act", lambda e: e.activation(out=es, in_=sinkB, func=AF.Exp), reads=["par"], writes=["es"])
        for j in range(NKT):
            S.add("dve", lambda e, j=j: e.tensor_scalar(out=vones[:, j, :], in0=onesb, scalar1=kvalid[:, j:j + 1],
                                                         scalar2=None, op0=ALU.mult),
                  reads=["par", "onesb"], writes=[("vones", j)])

        def rstd_from(out_ap, in_ap, n, rkeys, wkeys):
            S.add("act", lambda e: e.activation(out=out_ap, in_=in_ap, func=AF.Ln, scale=1.0 / n, bias=epsT),
                  reads=list(rkeys) + ["epsT"], writes=wkeys)
            S.add("act", lambda e: e.activation(out=out_ap, in_=out_ap, func=AF.Exp, scale=-0.5), reads=wkeys, writes=wkeys)

        def dump(ap_f32_view, n):
            S.full_barrier()
            S.add("sp", lambda e: e.dma_start(out=dbg[:, 0:n], in_=ap_f32_view), dma="dbg")
            S.full_barrier()
            S.add("sp", lambda e: None)

        o_E = ARENA - (NH * 570 // 2 + 2 * 570)
        Eb = bfv(o_E, NH * 570).rearrange("p (h r q) -> p h r q", h=NH, r=5)
        abs_ = [f32v(o_E + NH * 570 // 2 + k * 570, 570) for k in range(2)]
        S = S_real if part in (0, 1) else S_null
        cur[0] = base
        o_U = alloc(64 * 512 // 2)
        U = bfv(o_U, 64 * 512).rearrange("p (i n) -> p i n", i=64)
        f_base = cur[0]
        o_wus = alloc(KC * 512)
        wus = f32v(o_wus, KC * 512).rearrange("p (c n) -> p c n", c=KC)
        o_wub = alloc(KC * 512 // 2)
        wub = bfv(o_wub, KC * 512).rearrange("p (c n) -> p c n", c=KC)
        o_xb = [alloc(KC * 512 // 2) for _ in range(2)]
        xb = [bfv(o, KC * 512).rearrange("p (c n) -> p c n", c=KC) for o in o_xb]
        o_sm = alloc(64 * 2)
        ssu = f32v(o_sm, 64)
        rsu = f32v(o_sm + 64, 64)
        o_dg = [alloc(128) for _ in range(2)]
        dgt = [f32v(o, 128) for o in o_dg]

        for q4 in range(4):
            S.add("sp", lambda e, q4=q4: e.dma_start(
                out=wus[:, 4 * q4:4 * q4 + 4, :],
                in_=w_u[512 * q4:512 * (q4 + 1), :].rearrange("(c p) n -> p c n", p=128)),
                writes=[("wus", q4)], dma="wus%d" % q4)
        for c in range(KC):
            fold_last = S.add("dve", lambda e, c=c: e.tensor_scalar(out=wub[:, c, :], in0=wus[:, c, :], scalar1=g1T[:, c:c + 1],
                                                                     scalar2=None, op0=ALU.mult),
                              reads=[("wus", c // 4), "par"], writes=[("wub", c)])

        xstg = [wus[:, :, 0:256], wus[:, :, 256:512]]

        def load_xb(b):
            sl = b % 2
            for hf in range(2):
                S.add("sp", lambda e, hf=hf: e.dma_start(
                    out=xstg[hf], in_=xT[:, b * 512 + hf * 256:b * 512 + (hf + 1) * 256].rearrange("(c p) n -> p c n", p=128)),
                    reads=[], writes=[("xstg", hf)], dma="xs%d" % hf,
                    extra=([fold_last] if (b == 0 and fold_last is not None) else []))
                if hf == 0:
                    S.add("dve", lambda e, hf=hf: e.tensor_copy(out=xb[sl][:, :, 0:256], in_=xstg[0]),
                          reads=[("xstg", 0)], writes=[("xb", sl)])
                else:
                    S.add("act", lambda e, hf=hf: e.activation(out=xb[sl][:, :, 256:512], in_=xstg[1], func=AF.Copy),
                          reads=[("xstg", 1)], writes=[("xbh", sl)])

        if "F1" in _DBG.get("skip", ()):
            S = S_null
        load_xb(0)
        def e_build(h):
            sl = h % 2
            S_real.add("pool", lambda e: e.dma_start(out=abs_[sl][0:QT_, :], in_=abias[:, h * 570:(h + 1) * 570]),
                       writes=[("abs", sl)], dma="abs%d" % sl)
            S_real.add("act", lambda e: e.activation(
                out=Eb[0:QT_, h, :, :], in_=abs_[sl][0:QT_, :].rearrange("p (r q) -> p r q", r=5), func=AF.Exp),
                reads=[("abs", sl)], writes=[("E", h)])

        e_done = set()
        for b in range(NB):
            if part == 0 and 2 <= b < 2 + NH:
                e_build(b - 2)
                e_done.add(b - 2)
            sl = b % 2
            for i4 in range(4):
                if i4 == 2 and b + 1 < NB:
                    load_xb(b + 1)
                i = b * 4 + i4
                pb = i % 2
                cols = slice(i4 * 128, (i4 + 1) * 128)
                for c in range(KC):
                    S.add("pe", lambda e, c=c, sl=sl, cols=cols, pb=pb: e.matmul(
                        ps[:, pb, 0:512], lhsT=xb[sl][:, c, cols], rhs=wub[:, c, :], start=(c == 0), stop=(c == KC - 1)),
                        reads=[(("xb", sl) if i4 < 2 else ("xbh", sl)), ("wub", c)], writes=[("ps", pb)])
                for c in range(KC):
                    S.add("pe", lambda e, c=c, sl=sl, cols=cols, pb=pb: e.matmul(
                        ps[:, 2 + pb, 0:128], lhsT=xb[sl][:, c, cols], rhs=xb[sl][:, c, cols],
                        start=(c == 0), stop=(c == KC - 1)),
                        reads=[(("xb", sl) if i4 < 2 else ("xbh", sl))], writes=[("ps", 2 + pb)])
                S.add("dve", lambda e, pb=pb: e.tensor_tensor(out=dgt[pb], in0=ps[:, 2 + pb, 0:128], in1=identf, op=ALU.mult),
                      reads=[("ps", 2 + pb), "identf"], writes=[("dgt", pb)])
                S.add("dve", lambda e, pb=pb, i=i: e.reduce_sum(out=ssu[:, i:i + 1], in_=dgt[pb], axis=AX.X),
                      reads=[("dgt", pb)], writes=[("ssu", i)])
                rstd_from(rsu[:, i:i + 1], ssu[:, i:i + 1], float(D), [("ssu", i)], [("rsu", i)])
                S.add("act", lambda e, pb=pb, i=i: e.activation(out=U[:, i, :], in_=ps[:, pb, 0:512], func=AF.Copy,
                                                                 scale=rsu[:, i:i + 1]),
                      reads=[("ps", pb), ("rsu", i)], writes=[("U", i)])

        if _STAGE and _STAGE["name"] == "U":
            dump(arena[:, o_U:o_U + _STAGE["n"]], _STAGE["n"])

        S = S_real if part in (0, 1) else S_null
        if "F2" in _DBG.get("skip", ()):
            S = S_null
        S.full_barrier()
        cur[0] = f_base
        o_tab = [alloc(4 * 684 // 2) for _ in range(3)]
        tabs = [bfv(o, 4 * 684).rearrange("p (k n) -> p k n", k=4) for o in o_tab]
        o_AB = alloc(4 * 2 * T // 2)
        ABT = bfv(o_AB, 4 * 2 * T).rearrange("p (g a t) -> p g a t", g=4, a=2)
        o_ys = alloc(4 * T)
        ysb = f32v(o_ys, 4 * T).rearrange("p (g t) -> p g t", g=4)
        o_sqf = alloc(4 * T // 2)
        sqf = bfv(o_sqf, 4 * T).rearrange("p (g t) -> p g t", g=4)
        o_rf = alloc(T)
        rstdf = f32v(o_rf, T)
        o_mc = alloc(4 * 2 * 128 // 2)
        Mc = bfv(o_mc, 4 * 2 * 128).rearrange("p (g a n) -> p g a n", g=4, a=2)
        o_wf = alloc(4 * 128 // 2)
        wfb = bfv(o_wf, 4 * 128).rearrange("p (g n) -> p g n", g=4)
        o_cc = alloc(2 * 128 // 2)
        ccb = bfv(o_cc, 2 * 128).rearrange("p (a n) -> p a n", a=2)


        ntab = 0

        def load_tab(p, q):
            nonlocal ntab
            sl = ntab % 3
            ntab += 1
            S.add("sp", lambda e: e.dma_start(
                out=tabs[sl], in_=tab[p, q * 512:(q + 1) * 512, :].rearrange("(k s) n -> s k n", s=128)),
                writes=[("tab", sl)], dma="tab%d" % sl)
            return sl

        seq = [(p, q) for p in range(3) for q in range(NQ)]
        slots = {}
        slots[0] = load_tab(*seq[0])
        slots[1] = load_tab(*seq[1])
        for n_, (p, q) in enumerate(seq):
            if n_ + 2 < len(seq):
                slots[n_ + 2] = load_tab(*seq[n_ + 2])
            sl = slots[n_]
            lo, hi = TT[p]
            for k in range(4):
                i = q * 4 + k
                for g in range(4):
                    for a in range(2):
                        bk = g * 2 + a
                        S.add("pe", lambda e, i=i, g=g, a=a, bk=bk, sl=sl, k=k: e.matmul(
                            ps[:, bk, 0:342], lhsT=U[:, i, g * 128:(g + 1) * 128], rhs=tabs[sl][:, k, a * 342:(a + 1) * 342],
                            start=(i == 0), stop=(i == 4 * NQ - 1)),
                            reads=[("U", i), ("tab", sl)], writes=[("ps", bk)])
            if q == NQ - 1:
                for g in range(4):
                    for a in range(2):
                        bk = g * 2 + a
                        eng = "act" if a == 0 else "dve"
                        if eng == "act":
                            S.add("act", lambda e, g=g, a=a, bk=bk, lo=lo, hi=hi: e.activation(
                                out=ABT[:, g, a, lo:hi], in_=ps[:, bk, 0:342], func=AF.Copy),
                                reads=[("ps", bk)], writes=[("ABT", g, a, p)])
                        else:
                            S.add("dve", lambda e, g=g, a=a, bk=bk, lo=lo, hi=hi: e.tensor_copy(
                                out=ABT[:, g, a, lo:hi], in_=ps[:, bk, 0:342]),
                                reads=[("ps", bk)], writes=[("ABT", g, a, p)])

        S = S_real if part in (0, 1) else S_null
        if "F3" in _DBG.get("skip", ()):
            S = S_null
        if "f3n" in _DBG and part == 1:
            S = LimitSched(S_real, _DBG["f3n"])
        S.add("pool", lambda e: e.dma_start(out=wfb, in_=w_fou.rearrange("g k n -> k g n")), writes=["wfb"], dma="wfb")
        S.add("sp", lambda e: e.dma_start(out=ccb, in_=cc128.rearrange("p (a n) -> p a n", a=2)), writes=["ccb"], dma="ccb")
        for g in range(4):
            for a in range(2):
                S.add("pe", lambda e, g=g, a=a: e.matmul(ps[:, a, 0:128], lhsT=ccb[:, a, :], rhs=wfb[:, g, :],
                                                         start=True, stop=True),
                      reads=["ccb", "wfb"], writes=[("ps", a)])
                S.add("act", lambda e, g=g, a=a: e.activation(out=Mc[:, g, a, :], in_=ps[:, a, 0:128], func=AF.Copy,
                                                              scale=(1.0 if a == 0 else -1.0)),
                      reads=[("ps", a)], writes=[("Mc", g, a)])
        for p, (lo, hi) in enumerate(TT):
            for g in range(4):
                bk = 2 + (g % 2)
                for a in range(2):
                    S.add("pe", lambda e, g=g, a=a, bk=bk, lo=lo, hi=hi: e.matmul(
                        ps[:, bk, 0:342], lhsT=Mc[:, g, a, :], rhs=ABT[:, g, a, lo:hi], start=(a == 0), stop=(a == 1)),
                        reads=[("Mc", g, a), ("ABT", g, a, p)], writes=[("ps", bk)])
                S.add("dve", lambda e, g=g, bk=bk, lo=lo, hi=hi: e.tensor_copy(out=ysb[:, g, lo:hi], in_=ps[:, bk, 0:342]),
                      reads=[("ps", bk)], writes=[("ysb", g, p)])
                S.add("dve", lambda e, g=g, lo=lo, hi=hi: e.tensor_tensor(out=sqf[:, g, lo:hi], in0=ysb[:, g, lo:hi],
                                                                           in1=ysb[:, g, lo:hi], op=ALU.mult),
                      reads=[("ysb", g, p)], writes=[("sqf", g, p)])
            for g in range(4):
                S.add("pe", lambda e, g=g, lo=lo, hi=hi, p=p: e.matmul(ps[:, 4 + p, 0:342], lhsT=onesb, rhs=sqf[:, g, lo:hi],
                                                                       start=(g == 0), stop=(g == 3)),
                      reads=[("sqf", g, p), "onesb"], writes=[("ps", 4 + p)])
            rstd_from(rstdf[:, lo:hi], ps[:, 4 + p, 0:342], 512.0, [("ps", 4 + p)], [("rstdf", p)])
            for g in range(4):
                S.add("dve", lambda e, g=g, lo=lo, hi=hi: e.scalar_tensor_tensor(
                    out=mixT[:, 12 + g, lo:hi], in0=ysb[:, g, lo:hi], scalar=gfoT[:, g:g + 1], in1=rstdf[:, lo:hi],
                    op0=ALU.mult, op1=ALU.mult),
                    reads=[("ysb", g, p), ("rstdf", p), "par"], writes=[("mix", 12 + g, p)])

        if _STAGE and _STAGE["name"] == "F":
            dump(arena[:, o_mix:o_mix + _STAGE["n"]], _STAGE["n"])

        mixf = arena[:, o_mix + 12 * (T // 2):o_mix + 16 * (T // 2)].bitcast(BF16)
        if part == 1:
            S = S_real
            S.full_barrier()
            S.add("sp", lambda e: e.dma_start(out=fT_o, in_=mixf), dma="fTo")
            S.full_barrier()
            S.add("sp", lambda e: None)
        S = S_real if part in (0, 2) else S_null
        if part == 2:
            S.add("sp", lambda e: e.dma_start(out=mixf, in_=fT_i), writes=[("mix", 12 + g, p) for g in range(4) for p in range(3)],
                  dma="fTi")
        S.full_barrier()
        cur[0] = base
        o_QT = alloc(NH * T // 2)
        QTs = bfv(o_QT, NH * T).rearrange("p (h t) -> p h t", h=NH)
        o_KT = alloc(NKV * XH // 2)
        KTs = bfv(o_KT, NKV * XH).rearrange("p (h t) -> p h t", h=NKV)
        o_V = alloc(NKT * 512 // 2)
        Vs = bfv(o_V, NKT * 512).rearrange("p (j n) -> p j n", j=NKT)
        att_base = cur[0]
        o_xg = alloc(KC * XH // 2)
        xg = bfv(o_xg, KC * XH).rearrange("p (c t) -> p c t", c=KC)
        o_r1 = alloc(XH)
        rstd1 = f32v(o_r1, XH)
        o_VT = alloc(NKV * XH // 2)
        VTs = bfv(o_VT, NKV * XH).rearrange("p (h t) -> p h t", h=NKV)
        o_xhs = [alloc(XH) for _ in range(2)]
        xhs = [f32v(o, XH) for o in o_xhs]
        o_sqn = [alloc(XH // 2) for _ in range(2)]
        sqn = [bfv(o, XH) for o in o_sqn]
        o_wq = [alloc(KC * 256 // 2) for _ in range(2)]
        wq = [bfv(o, KC * 256).rearrange("p (c n) -> p c n", c=KC) for o in o_wq]
        XT3 = [(0, 494), (494, 988), (988, 1482)]

        def load_wq(n_):
            sl = n_ % 2
            S.add("pool", lambda e: e.dma_start(
                out=wq[sl], in_=w_qkv[:, n_ * 256:(n_ + 1) * 256].rearrange("(c p) n -> p c n", p=128)),
                writes=[("wq", sl)], dma="wq%d" % sl)

        load_wq(0)
        load_wq(1)
        for c in range(KC):
            sl = c % 2
            S.add("sp", lambda e, c=c, sl=sl: e.dma_start(out=xhs[sl], in_=xh[c * 128:(c + 1) * 128, :]),
                  writes=[("xhs", sl)], dma="xhs%d" % sl)
            S.add("act", lambda e, c=c, sl=sl: e.activation(out=xg[:, c, :], in_=xhs[sl], func=AF.Copy,
                                                             scale=g1T[:, c:c + 1]),
                  reads=[("xhs", sl), "par"], writes=[("xg", c)])
            S.add("dve", lambda e, c=c, sl=sl: e.tensor_tensor(out=sqn[sl], in0=xhs[sl], in1=xhs[sl], op=ALU.mult),
                  reads=[("xhs", sl)], writes=[("sqn", sl)])
            for n_, (lo, hi) in enumerate(XT3):
                S.add("pe", lambda e, c=c, sl=sl, n_=n_, lo=lo, hi=hi: e.matmul(
                    ps[:, n_, 0:494], lhsT=onesb, rhs=sqn[sl][:, lo:hi], start=(c == 0), stop=(c == KC - 1)),
                    reads=[("sqn", sl), "onesb"], writes=[("ps", n_)])
        for n_, (lo, hi) in enumerate(XT3):
            rstd_from(rstd1[:, lo:hi], ps[:, n_, 0:494], float(D), [("ps", n_)], [("rstd1", n_)])

        for cc in range(20):
            pair = cc // 2
            sl = pair % 2
            off = (cc % 2) * 128
            if cc < 12:
                tiles = [(QOFF + lo, QOFF + hi) for (lo, hi) in TT]
            else:
                tiles = XT3
            bb = 3 * (cc % 2)
            for n_, (lo, hi) in enumerate(tiles):
                w = hi - lo
                for c in range(KC):
                    S.add("pe", lambda e, c=c, sl=sl, off=off, bb=bb, n_=n_, lo=lo, hi=hi, w=w: e.matmul(
                        ps[:, bb + n_, 0:w], lhsT=wq[sl][:, c, off:off + 128], rhs=xg[:, c, lo:hi],
                        start=(c == 0), stop=(c == KC - 1)),
                        reads=[("wq", sl), ("xg", c)], writes=[("ps", bb + n_)])
            for n_, (lo, hi) in enumerate(tiles):
                w = hi - lo
                if cc < 12:
                    dst = QTs[:, cc, lo - QOFF:hi - QOFF]
                    wk = [("QT", cc, n_)]
                elif cc < 16:
                    dst = KTs[:, cc - 12, lo:hi]
                    wk = [("KT", cc - 12, n_)]
                else:
                    dst = VTs[:, cc - 16, lo:hi]
                    wk = [("VT", cc - 16, n_)]
                rk = [("ps", bb + n_), ("rstd1", 0), ("rstd1", 1), ("rstd1", 2)]
                S.add("dve", lambda e, dst=dst, bb=bb, n_=n_, lo=lo, hi=hi, w=w: e.tensor_tensor(
                    out=dst, in0=ps[:, bb + n_, 0:w], in1=rstd1[:, lo:hi], op=ALU.mult), reads=rk, writes=wk)
            if cc % 2 == 1 and pair + 2 < 10:
                load_wq(pair + 2)

        psb = [ps[:, 6, :].bitcast(BF16), ps[:, 7, :].bitcast(BF16)]
        for j in range(NKT):
            pb = j % 2
            for h in range(NKV):
                S.add("pe", lambda e, j=j, h=h, pb=pb: e.transpose(
                    psb[pb][0:QT_, h * 128:(h + 1) * 128], VTs[:, h, j * QT_:(j + 1) * QT_], identb),
                    reads=[("VT", h, 0), ("VT", h, 1), ("VT", h, 2), "identb"], writes=[("ps", 6 + pb)])
            S.add("act", lambda e, j=j, pb=pb: e.activation(out=Vs[0:QT_, j, :], in_=psb[pb][0:QT_, 0:512], func=AF.Copy),
                  reads=[("ps", 6 + pb)], writes=[("V", j)])

        if _STAGE and _STAGE["name"] == "QKV":
            dump(arena[:, o_QT:o_QT + _STAGE["n"]], _STAGE["n"])

        S.full_barrier()
        cur[0] = att_base
        o_att = alloc(NH * T)
        attT = f32v(o_att, NH * T).rearrange("p (h t) -> p h t", h=NH)
        o_ex = [alloc(171) for _ in range(4)]
        exs = [bfv(o, 342) for o in o_ex]
        o_pt = [alloc(171) for _ in range(4)]
        pts = [bfv(o, 342) for o in o_pt]
        assert cur[0] <= o_E
        o_dn = [alloc(342) for _ in range(2)]
        dns = [f32v(o, 342) for o in o_dn]
        o_sqa = [alloc(T // 2) for _ in range(2)]
        sqa = [bfv(o, T) for o in o_sqa]
        o_ra = alloc(T)
        rstda = f32v(o_ra, T)

        if part in (0, 2):
            for h in range(NH):
                if h not in e_done:
                    e_build(h)
        sc = 128.0 ** -0.5
        LOOK = 3
        steps = [(i, kvh, r) for i in range(NQT) for kvh in range(NKV) for r in range(5)]

        def emit_S(n):
            i, kvh, r = steps[n]
            j = i + r
            sb_ = n % 4
            kc = slice(j * QT_, (j + 1) * QT_)
            qc = slice(i * QT_, (i + 1) * QT_)
            S.add("pe", lambda e: e.matmul(
                ps[0:QT_, sb_, 0:342].rearrange("p (h q) -> p h q", h=3),
                lhsT=KTs[:, kvh, kc], rhs=QTs[:, 3 * kvh:3 * kvh + 3, qc], start=True, stop=True),
                reads=[("KT", kvh, 0), ("KT", kvh, 1), ("KT", kvh, 2)] +
                      [("QT", 3 * kvh + hh, n_) for hh in range(3) for n_ in range(3)],
                writes=[("ps", sb_)])

        def emit_norm(i, kvh, ucnt, qc, ob, db):
            dsl = ucnt % 2
            for hh in range(3):
                h = 3 * kvh + hh
                S.add("act", lambda e, hh=hh, h=h, db=db, dsl=dsl: e.activation(
                    out=dns[dsl][:, hh * QT_:(hh + 1) * QT_], in_=ps[:, db, hh * QT_:(hh + 1) * QT_],
                    func=AF.Ln, bias=es[:, h:h + 1]),
                    reads=[("ps", db), "es"], writes=[("dn", dsl, hh)])
            S.add("act", lambda e, dsl=dsl: e.activation(out=dns[dsl], in_=dns[dsl], func=AF.Exp, scale=-1.0),
                  reads=[("dn", dsl, 0), ("dn", dsl, 1), ("dn", dsl, 2)], writes=[("rc", dsl)])
            S.add("dve", lambda e, kvh=kvh, qc=qc, ob=ob, dsl=dsl: e.tensor_tensor(
                out=attT[:, 3 * kvh:3 * kvh + 3, qc],
                in0=ps[:, ob, 0:342].rearrange("p (h q) -> p h q", h=3),
                in1=dns[dsl].rearrange("p (h q) -> p h q", h=3), op=ALU.mult),
                reads=[("ps", ob), ("rc", dsl)], writes=[("att", 3 * kvh + hh, i) for hh in range(3)])

        pending_norm = []
        for n in range(min(LOOK, len(steps))):
            emit_S(n)
        for n, (i, kvh, r) in enumerate(steps):
            if n + LOOK < len(steps):
                emit_S(n + LOOK)
            ucnt = n // 5
            qc = slice(i * QT_, (i + 1) * QT_)
            ob = 4 + (ucnt % 2)
            db = 6 + (ucnt % 2)
            j = i + r
            sb_ = n % 4
            es_ = n % 4
            S.add("act", lambda e, sb_=sb_, es_=es_: e.activation(
                out=exs[es_][0:QT_, :], in_=ps[0:QT_, sb_, 0:342], func=AF.Exp, scale=sc),
                reads=[("ps", sb_)], writes=[("ex", es_)])
            S.add("dve", lambda e, es_=es_, kvh=kvh, r=r: e.tensor_tensor(
                out=pts[es_][0:QT_, :].rearrange("p (h q) -> p h q", h=3),
                in0=exs[es_][0:QT_, :].rearrange("p (h q) -> p h q", h=3),
                in1=Eb[0:QT_, 3 * kvh:3 * kvh + 3, r, :], op=ALU.mult),
                reads=[("ex", es_)] + [("E", 3 * kvh + hh) for hh in range(3)], writes=[("pt", es_)])
            S.add("pe", lambda e, j=j, kvh=kvh, es_=es_, ob=ob, r=r: e.matmul(
                ps[:, ob, 0:342], lhsT=Vs[0:QT_, j, kvh * 128:(kvh + 1) * 128], rhs=pts[es_][0:QT_, :],
                start=(r == 0), stop=(r == 4)),
                reads=[("V", j), ("pt", es_)], writes=[("ps", ob)])
            S.add("pe", lambda e, j=j, es_=es_, db=db, r=r: e.matmul(
                ps[:, db, 0:342], lhsT=vones[0:QT_, j, :], rhs=pts[es_][0:QT_, :],
                start=(r == 0), stop=(r == 4)),
                reads=[("vones", j), ("pt", es_)], writes=[("ps", db)])
            for rel_n, fn_ in list(pending_norm):
                if rel_n <= n:
                    fn_()
                    pending_norm.remove((rel_n, fn_))
            if r == 4:
                pending_norm.append((n + 2, (lambda i=i, kvh=kvh, ucnt=ucnt, qc=qc, ob=ob, db=db: emit_norm(i, kvh, ucnt, qc, ob, db))))
        for rel_n, fn_ in pending_norm:
            fn_()

        for h in range(NH):
            sl = h % 2
            S.add("dve", lambda e, h=h, sl=sl: e.tensor_tensor(out=sqa[sl], in0=attT[:, h, :], in1=attT[:, h, :], op=ALU.mult),
                  reads=[("att", h, i) for i in range(NQT)], writes=[("sqa", sl)])
            for n_, (lo, hi) in enumerate(TT):
                S.add("pe", lambda e, h=h, sl=sl, n_=n_, lo=lo, hi=hi: e.matmul(
                    ps[:, n_, 0:342], lhsT=onesb, rhs=sqa[sl][:, lo:hi], start=(h == 0), stop=(h == NH - 1)),
                    reads=[("sqa", sl), "onesb"], writes=[("ps", n_)])
        for n_, (lo, hi) in enumerate(TT):
            rstd_from(rstda[:, lo:hi], ps[:, n_, 0:342], 1536.0, [("ps", n_)], [("rstda", n_)])
        for h in range(NH):
            S.add("dve", lambda e, h=h: e.scalar_tensor_tensor(
                out=mixT[:, h, :], in0=attT[:, h, :], scalar=gaT[:, h:h + 1], in1=rstda,
                op0=ALU.mult, op1=ALU.mult),
                reads=[("att", h, i) for i in range(NQT)] + [("rstda", n_) for n_ in range(3)] + ["par"],
                writes=[("mix", h, p) for p in range(3)])

        if _STAGE and _STAGE["name"] == "ATT":
            dump(arena[:, o_mix:o_mix + _STAGE["n"]], _STAGE["n"])

        S.full_barrier()
        cur[0] = base
        o_x1 = alloc(KC * T)
        x1T = f32v(o_x1, KC * T).rearrange("p (c t) -> p c t", c=KC)
        o_h2 = alloc(KC * T // 2)
        h2T = bfv(o_h2, KC * T).rearrange("p (c t) -> p c t", c=KC)
        ffn_base = cur[0]
        o_wo = [alloc(KC * 256 // 2) for _ in range(2)]
        wo = [bfv(o, KC * 256).rearrange("p (c n) -> p c n", c=KC) for o in o_wo]
        o_xr = [alloc(T) for _ in range(2)]
        xr = [f32v(o, T) for o in o_xr]
        o_sq2 = [alloc(T // 2) for _ in range(2)]
        sq2 = [bfv(o, T) for o in o_sq2]
        o_r2 = alloc(T)
        rstd2 = f32v(o_r2, T)
        o_sqall = alloc(KC * T // 2)
        sqall = bfv(o_sqall, KC * T).rearrange("p (c t) -> p c t", c=KC)

        def load_wo(n_):
            sl = n_ % 2
            S.add("pool", lambda e: e.dma_start(
                out=wo[sl], in_=w_out[:, n_ * 256:(n_ + 1) * 256].rearrange("(c p) n -> p c n", p=128)),
                writes=[("wo", sl)], dma="wo%d" % sl)

        load_wo(0)
        load_wo(1)
        for m in range(KC):
            pair = m // 2
            sl = pair % 2
            off = (m % 2) * 128
            bb = 3 * (m % 2)
            xs = m % 2
            S.add("sp", lambda e, m=m, xs=xs: e.dma_start(out=xr[xs], in_=xh[m * 128:(m + 1) * 128, QOFF:QOFF + T]),
                  writes=[("xr", xs)], dma="xr%d" % xs)
            for n_, (lo, hi) in enumerate(TT):
                for k in range(KC):
                    S.add("pe", lambda e, k=k, sl=sl, off=off, bb=bb, n_=n_, lo=lo, hi=hi: e.matmul(
                        ps[:, bb + n_, 0:342], lhsT=wo[sl][:, k, off:off + 128], rhs=mixT[:, k, lo:hi],
                        start=(k == 0), stop=(k == KC - 1)),
                        reads=[("wo", sl), ("mix", k, n_)], writes=[("ps", bb + n_)])
            for n_, (lo, hi) in enumerate(TT):
                S.add("dve", lambda e, m=m, bb=bb, n_=n_, lo=lo, hi=hi, xs=xs: e.tensor_tensor(
                    out=x1T[:, m, lo:hi], in0=ps[:, bb + n_, 0:342], in1=xr[xs][:, lo:hi], op=ALU.add),
                    reads=[("ps", bb + n_), ("xr", xs)], writes=[("x1", m, n_)])
            S.add("dve", lambda e, m=m: e.tensor_tensor(out=sqall[:, m, :], in0=x1T[:, m, :], in1=x1T[:, m, :], op=ALU.mult),
                  reads=[("x1", m, n_) for n_ in range(3)], writes=[("sqall", m)])
            if m % 2 == 1 and pair + 2 < 8:
                load_wo(pair + 2)

        def rms_stats(src_fn, ncols, tiles, sqbuf, key_src_fn, outr, outkey, nfeat):
            for c in range(KC):
                sl = c % 2
                S.add("dve", lambda e, c=c, sl=sl: e.tensor_tensor(out=sqbuf[sl][:, 0:ncols], in0=src_fn(c), in1=src_fn(c), op=ALU.mult),
                      reads=key_src_fn(c), writes=[("sqb", sl)])
                for n_, (lo, hi) in enumerate(tiles):
                    S.add("pe", lambda e, c=c, sl=sl, n_=n_, lo=lo, hi=hi: e.matmul(
                        ps[:, n_, 0:hi - lo], lhsT=onesb, rhs=sqbuf[sl][:, lo:hi], start=(c == 0), stop=(c == KC - 1)),
                        reads=[("sqb", sl), "onesb"], writes=[("ps", n_)])
            for n_, (lo, hi) in enumerate(tiles):
                rstd_from(outr[:, lo:hi], ps[:, n_, 0:hi - lo], nfeat, [("ps", n_)], [(outkey, n_)])

        for c in range(KC):
            for n_, (lo, hi) in enumerate(TT):
                S.add("pe", lambda e, c=c, n_=n_, lo=lo, hi=hi: e.matmul(
                    ps[:, n_, 0:hi - lo], lhsT=onesb, rhs=sqall[:, c, lo:hi], start=(c == 0), stop=(c == KC - 1)),
                    reads=[("sqall", c), "onesb"], writes=[("ps", n_)])
        for n_, (lo, hi) in enumerate(TT):
            rstd_from(rstd2[:, lo:hi], ps[:, n_, 0:hi - lo], float(D), [("ps", n_)], [("rstd2", n_)])
        for c in range(KC):
            S.add("dve", lambda e, c=c: e.scalar_tensor_tensor(
                out=h2T[:, c, :], in0=x1T[:, c, :], scalar=g2T[:, c:c + 1], in1=rstd2, op0=ALU.mult, op1=ALU.mult),
                reads=[("x1", c, n_) for n_ in range(3)] + [("rstd2", n_) for n_ in range(3)] + ["par"],
                writes=[("h2", c)])
        S.add("dve", lambda e: e.tensor_scalar(out=h2T[:, :, 0:1], in0=h2T[:, :, 0:1], scalar1=qv[:, 0:1], scalar2=None,
                                               op0=ALU.mult),
              reads=[("h2", c) for c in range(KC)] + ["par"], writes=[("h2", c) for c in range(KC)])
        S.add("dve", lambda e: e.tensor_scalar(out=h2T[:, :, T - 1:T], in0=h2T[:, :, T - 1:T], scalar1=qv[:, 1:2],
                                               scalar2=None, op0=ALU.mult),
              reads=[("h2", c) for c in range(KC)] + ["par"], writes=[("h2", c) for c in range(KC)])

        if _STAGE and _STAGE["name"] == "X1":
            dump(arena[:, o_x1:o_x1 + _STAGE["n"]], _STAGE["n"])

        S.full_barrier()
        cur[0] = ffn_base
        GMAX = max(FGROUPS)
        o_wg = [alloc(KC * 256 // 2) for _ in range(2)]
        wg = [bfv(o, KC * 256).rearrange("p (c n) -> p c n", c=KC) for o in o_wg]
        o_wv = [alloc(KC * 256 // 2) for _ in range(2)]
        wv = [bfv(o, KC * 256).rearrange("p (c n) -> p c n", c=KC) for o in o_wv]
        o_act = [alloc(GMAX * TOK // 2) for _ in range(2)]
        actb = [bfv(o, GMAX * TOK).rearrange("p (f t) -> p f t", f=GMAX) for o in o_act]
        _save = cur[0]
        cur[0] = o_mix
        o_wd = [alloc(GMAX * 512 // 2) for _ in range(2)]
        wd = [bfv(o, GMAX * 512).rearrange("p (f n) -> p f n", f=GMAX) for o in o_wd]
        o_t1 = [alloc(342) for _ in range(3)]
        t1s = [f32v(o, 342) for o in o_t1]
        o_ge = [alloc(342) for _ in range(3)]
        ges = [f32v(o, 342) for o in o_ge]
        assert cur[0] <= o_mix + KC * T // 2
        cur[0] = _save
        GT = [(0, 344), (342, 686), (684, 1026)]
        OT = [(1, 343), (343, 685), (685, 1025)]

        npairs = (NFF + 1) // 2
        f_lo = {2: 0, 3: FSPLIT}.get(part, 0)
        pair_hi = {0: npairs, 2: FSPLIT // 2, 3: npairs}.get(part, 0)

        def load_wup(pi):
            if pi >= pair_hi:
                return
            sl = pi % 2
            f0 = pi * 2 - f_lo
            nf = min(2, NFF - pi * 2)
            S.add("pool", lambda e: e.dma_start(
                out=wg[sl][:, :, 0:nf * 128],
                in_=w_upg[:, f0 * 128:(f0 + nf) * 128].rearrange("(c p) n -> p c n", p=128)),
                writes=[("wg", sl)], dma="wg%d" % sl)
            S.add("pool", lambda e: e.dma_start(
                out=wv[sl][:, :, 0:nf * 128],
                in_=w_upv[:, f0 * 128:(f0 + nf) * 128].rearrange("(c p) n -> p c n", p=128)),
                writes=[("wv", sl)], dma="wv%d" % sl)

        nwd = 0

        def load_wd(f0, gsz, q):
            nonlocal nwd
            sl = nwd % 2
            nwd += 1
            S.add("pool", lambda e: e.dma_start(
                out=wd[sl][:, 0:gsz, :],
                in_=w_dn[(f0 - f_lo) * 128:(f0 - f_lo + gsz) * 128, q * 512:(q + 1) * 512].rearrange("(f p) n -> p f n", p=128)),
                writes=[("wd", sl)], dma="wd%d" % sl)
            return sl

        f = 0
        dcnt = 0
        assert sum(FGROUPS[:2]) == FSPLIT and FSPLIT % 2 == 0
        for gi, gsz in enumerate(FGROUPS):
            if gi == 0:
                S = S_real if part in (0, 2) else S_null
                load_wup(0)
                load_wup(1)
            if gi == 2:
                if part == 2:
                    S.full_barrier()
                    S.add("sp", lambda e: e.dma_start(out=x1_o, in_=arena[:, o_x1:o_x1 + KC * T]), dma="x1o")
                    S.add("sp", lambda e: e.dma_start(out=h2_o, in_=arena[:, o_h2:o_h2 + KC * T // 2].bitcast(BF16)), dma="h2o")
                    S.full_barrier()
                    S.add("sp", lambda e: None)
                S = S_real if part in (0, 3) else S_null
                if part == 3:
                    S.add("sp", lambda e: e.dma_start(out=arena[:, o_x1:o_x1 + KC * T], in_=x1_i),
                          writes=[("x2", m, n_) for m in range(KC) for n_ in range(2)], dma="x1i")
                    S.add("sp", lambda e: e.dma_start(out=arena[:, o_h2:o_h2 + KC * T // 2].bitcast(BF16), in_=h2_i),
                          writes=[("h2", c) for c in range(KC)], dma="h2i")
                    S.full_barrier()
                    load_wup(FSPLIT // 2)
                    load_wup(FSPLIT // 2 + 1)
            asl = gi % 2
            f0g = f
            wd_pre = (load_wd(f0g, gsz, 0), load_wd(f0g, gsz, 1))
            for fi in range(gsz):
                pi = f // 2
                sl = pi % 2
                off = (f % 2) * 128
                for n_, (lo, hi) in enumerate(GT):
                    for k in range(KC):
                        S.add("pe", lambda e, k=k, sl=sl, off=off, n_=n_, lo=lo, hi=hi: e.matmul(
                            ps[:, n_, 0:hi - lo], lhsT=wg[sl][:, k, off:off + 128], rhs=h2T[:, k, lo:hi],
                            start=(k == 0), stop=(k == KC - 1)),
                            reads=[("wg", sl), ("h2", k)], writes=[("ps", n_)])
                for n_, (lo, hi) in enumerate(OT):
                    for k in range(KC):
                        S.add("pe", lambda e, k=k, sl=sl, off=off, n_=n_, lo=lo, hi=hi: e.matmul(
                            ps[:, 3 + n_, 0:hi - lo], lhsT=wv[sl][:, k, off:off + 128], rhs=h2T[:, k, lo:hi],
                            start=(k == 0), stop=(k == KC - 1)),
                            reads=[("wv", sl), ("h2", k)], writes=[("ps", 3 + n_)])
                if f % 2 == 1 and pi + 2 < npairs:
                    load_wup(pi + 2)
                for n_, (lo, hi) in enumerate(OT):
                    w = hi - lo
                    S.add("act", lambda e, f=f, n_=n_, w=w: e.activation(
                        out=t1s[n_][:, 0:w], in_=ps[:, n_, 1:1 + w], func=AF.Identity,
                        scale=dwT[:, f, 1:2], bias=dwT[:, f, 3:4]),
                        reads=[("ps", n_), "par"], writes=[("t1", n_)])
                for n_, (lo, hi) in enumerate(OT):
                    w = hi - lo
                    S.add("dve", lambda e, f=f, n_=n_, w=w: e.scalar_tensor_tensor(
                        out=t1s[n_][:, 0:w], in0=ps[:, n_, 0:w], scalar=dwT[:, f, 0:1], in1=t1s[n_][:, 0:w],
                        op0=ALU.mult, op1=ALU.add),
                        reads=[("ps", n_), ("t1", n_), "par"], writes=[("t1", n_)])
                for n_, (lo, hi) in enumerate(OT):
                    w = hi - lo
                    S.add("dve", lambda e, f=f, n_=n_, w=w: e.scalar_tensor_tensor(
                        out=t1s[n_][:, 0:w], in0=ps[:, n_, 2:2 + w], scalar=dwT[:, f, 2:3], in1=t1s[n_][:, 0:w],
                        op0=ALU.mult, op1=ALU.add),
                        reads=[("ps", n_), ("t1", n_), "par"], writes=[("t1", n_)])
                for n_, (lo, hi) in enumerate(OT):
                    w = hi - lo
                    S.add("act", lambda e, n_=n_, w=w: e.activation(out=ges[n_][:, 0:w], in_=t1s[n_][:, 0:w], func=AF.Gelu),
                          reads=[("t1", n_)], writes=[("ge", n_)])
                for n_, (lo, hi) in enumerate(OT):
                    w = hi - lo
                    S.add("dve", lambda e, n_=n_, w=w, fi=fi, lo=lo, hi=hi, asl=asl: e.tensor_tensor(
                        out=actb[asl][:, fi, lo - 1:hi - 1], in0=ges[n_][:, 0:w], in1=ps[:, 3 + n_, 0:w], op=ALU.mult),
                        reads=[("ge", n_), ("ps", 3 + n_)], writes=[("act", asl, fi)])
                f += 1
            wsl, wsl1 = wd_pre
            for q in range(4):
                if q == 0:
                    nxt = wsl1
                else:
                    nxt = load_wd(f0g, gsz, q + 1) if q + 1 < 4 else None
                for mm in range(4):
                    m = q * 4 + mm
                    for n_ in range(2):
                        bk = 6 + (dcnt % 2)
                        dcnt += 1
                        for fi in range(gsz):
                            S.add("pe", lambda e, fi=fi, wsl=wsl, mm=mm, n_=n_, bk=bk, asl=asl, gsz=gsz: e.matmul(
                                ps[:, bk, 0:512], lhsT=wd[wsl][:, fi, mm * 128:(mm + 1) * 128],
                                rhs=actb[asl][:, fi, n_ * 512:(n_ + 1) * 512], start=(fi == 0), stop=(fi == gsz - 1)),
                                reads=[("wd", wsl), ("act", asl, fi)], writes=[("ps", bk)])
                        S.add("dve", lambda e, m=m, n_=n_, bk=bk: e.tensor_tensor(
                            out=x1T[:, m, 1 + n_ * 512:1 + (n_ + 1) * 512], in0=ps[:, bk, 0:512],
                            in1=x1T[:, m, 1 + n_ * 512:1 + (n_ + 1) * 512], op=ALU.add),
                            reads=[("ps", bk), ("x2", m, n_)], writes=[("x2", m, n_)])
                wsl = nxt

        if _STAGE and _STAGE["name"] == "X2":
            dump(arena[:, o_x1:o_x1 + _STAGE["n"]], _STAGE["n"])

        S.full_barrier()
        cur[0] = ffn_base
        o_sq3 = [alloc(TOK // 2) for _ in range(2)]
        sq3 = [bfv(o, TOK) for o in o_sq3]
        o_r3 = alloc(TOK)
        rstd3 = f32v(o_r3, TOK)
        o_yo = [alloc(TOK) for _ in range(2)]
        yo = [f32v(o, TOK) for o in o_yo]
        T2 = [(0, 512), (512, 1024)]
        rms_stats(lambda c: x1T[:, c, 1:1 + TOK], TOK, T2, sq3,
                  lambda c: [("x2", c, 0), ("x2", c, 1)], rstd3, "rstd3", float(D))
        outs = []
        for c in range(KC):
            sl = c % 2
            S.add("dve", lambda e, c=c, sl=sl: e.scalar_tensor_tensor(
                out=yo[sl], in0=x1T[:, c, 1:1 + TOK], scalar=gnT[:, c:c + 1], in1=rstd3, op0=ALU.mult, op1=ALU.mult),
                reads=[("x2", c, 0), ("x2", c, 1), ("rstd3", 0), ("rstd3", 1), "par"], writes=[("yo", sl)])
            outs.append(S.add("sp", lambda e, c=c, sl=sl: e.dma_start(out=yT[c * 128:(c + 1) * 128, :], in_=yo[sl]),
                              reads=[("yo", sl)], dma="yo%d" % sl))
        S.full_barrier()
        S.add("sp", lambda e: None)

        def sem_alloc(name):
            return sem_pool.pop()

        S_real.emit(nc, block, sem_alloc)
    return nc


def _alibi_slopes(n_heads):
    def pow2_slopes(n):
        start = 2.0 ** (-8.0 / n)
        return [start ** (i + 1) for i in range(n)]
    if math.log2(n_heads).is_integer():
        s = pow2_slopes(n_heads)
    else:
        closest = 2 ** int(math.floor(math.log2(n_heads)))
        s = pow2_slopes(closest) + pow2_slopes(2 * closest)[0::2][: n_heads - closest]
    return np.array(s, dtype=np.float32)


_CONST_CACHE = {}


def _constants():
    if _CONST_CACHE:
        return _CONST_CACHE
    bf = ml_dtypes.bfloat16
    slopes = _alibi_slopes(NH)
    kk = np.arange(QT_)[:, None, None]
    r = np.arange(5)[None, :, None]
    qq = np.arange(QT_)[None, None, :]
    rel = QT_ * (r - 2) + kk - qq
    ab = np.empty((QT_, NH, 5, QT_), np.float32)
    for h in range(NH):
        ab[:, h] = np.where(np.abs(rel) <= 128, -slopes[h] * np.abs(rel).astype(np.float32), -30000.0)
    _CONST_CACHE["abias"] = ab.reshape(QT_, NH * 5 * QT_)
    k = np.arange(128)
    ang = 2.0 * np.pi * ((k[:, None] * k[None, :]) % 128) / 128.0
    cc = np.concatenate([np.cos(ang) / 1024.0, np.sin(ang) / 1024.0], axis=1)
    _CONST_CACHE["cc128"] = cc.astype(np.float32).astype(bf)
    s = np.arange(SEQ, dtype=np.int64)[:, None]
    tabs = []
    for c in range(NCORES):
        t = (np.arange(T, dtype=np.int64) + 1024 * c - 1) % SEQ
        ang = 2.0 * np.pi * ((s * t[None, :]) % SEQ).astype(np.float64) / SEQ
        co = np.cos(ang).astype(np.float32).astype(bf)
        si = np.sin(ang).astype(np.float32).astype(bf)
        tb = np.empty((3, SEQ, 2 * 342), bf)
        for p, (lo, hi) in enumerate(TT):
            tb[p, :, 0:342] = co[:, lo:hi]
            tb[p, :, 342:684] = si[:, lo:hi]
        tabs.append(tb)
    _CONST_CACHE["tab"] = tabs
    cst = np.zeros((128, 256), np.float32)
    cst[:, 0:128] = np.eye(128, dtype=np.float32)
    _CONST_CACHE["cst"] = cst
    return _CONST_CACHE


def _colmajor(v, n):
    return np.ascontiguousarray(np.asarray(v, np.float32).reshape(n, 128).T)


def _prep(inputs):
    C = _constants()
    x = np.asarray(inputs["x"], np.float32)[0]
    xT = np.ascontiguousarray(x.T)
    w_in = np.asarray(inputs["w_in"], np.float32)
    w_up = np.asarray(inputs["w_up"], np.float32)
    w_down = np.asarray(inputs["w_down"], np.float32)
    par0 = np.zeros((128, 320), np.float32)
    par0[:, 0:16] = _colmajor(inputs["norm1_g"], 16)
    par0[:, 16:32] = _colmajor(inputs["norm2_g"], 16)
    par0[:, 32:48] = _colmajor(inputs["normf_g"], 16)
    par0[:, 48:60] = _colmajor(inputs["attn_out_g"], 12)
    par0[:, 60:64] = _colmajor(inputs["fourier_out_g"], 4)
    par0[:, 64:76] = np.asarray(inputs["sink"], np.float32)[None, :]
    dw = np.asarray(inputs["dw_w"], np.float32)[:, 0, :]
    db = np.asarray(inputs["dw_b"], np.float32)
    dwp = np.zeros((128, 43, 4), np.float32)
    for k in range(3):
        dwp[:, :, k] = _colmajor(dw[k], 43)
    dwp[:, :, 3] = _colmajor(db, 43)
    par0[:, 96:96 + 172] = dwp.reshape(128, 172)
    fs = FSPLIT * 128
    sh = {
        "xT": xT,
        "w_u": np.ascontiguousarray(w_in[:, 2560:3072]),
        "w_qkv": np.ascontiguousarray(w_in[:, 0:2560]),
        "w_out": np.ascontiguousarray(np.asarray(inputs["w_out"], np.float32)),
        "w_fou": np.ascontiguousarray(np.asarray(inputs["w_fourier"], np.float32)),
        "upg2": np.ascontiguousarray(w_up[:, 0:fs]), "upv2": np.ascontiguousarray(w_up[:, DFF:DFF + fs]),
        "dn2": np.ascontiguousarray(w_down[0:fs, :]),
        "upg3": np.ascontiguousarray(w_up[:, fs:DFF]), "upv3": np.ascontiguousarray(w_up[:, DFF + fs:2 * DFF]),
        "dn3": np.ascontiguousarray(w_down[fs:, :]),
        "upg": np.ascontiguousarray(w_up[:, 0:DFF]), "upv": np.ascontiguousarray(w_up[:, DFF:2 * DFF]),
        "dn": np.ascontiguousarray(w_down),
    }
    pars, xhs = [], []
    for c in range(NCORES):
        t0 = 1024 * c - 1
        tok = t0 - QOFF + np.arange(XH)
        valid = (tok >= 0) & (tok < SEQ)
        xhc = np.zeros((D, XH), np.float32)
        xhc[:, valid] = xT[:, tok[valid]]
        par = par0.copy()
        par[0:QT_, 76:89] = valid.astype(np.float32).reshape(NKT, QT_).T
        par[:, 89] = 1.0 if t0 >= 0 else 0.0
        par[:, 90] = 1.0 if t0 + T - 1 < SEQ else 0.0
        pars.append(par)
        xhs.append(xhc)
    return C, sh, pars, xhs


def _maps1(C, sh, pars, xhs, c):
    return {"xT": sh["xT"], "w_u": sh["w_u"], "w_fou": sh["w_fou"], "tab": C["tab"][c], "cc128": C["cc128"],
            "par": pars[c], "cst": C["cst"]}


def _maps2(C, sh, pars, xhs, c, fT):
    return {"xh": xhs[c], "w_qkv": sh["w_qkv"], "w_out": sh["w_out"], "w_upg": sh["upg2"], "w_upv": sh["upv2"],
            "w_dn": sh["dn2"], "abias": C["abias"], "par": pars[c], "cst": C["cst"], "fT": fT}


def _maps3(C, sh, pars, xhs, c, x1, h2):
    return {"w_upg": sh["upg3"], "w_upv": sh["upv3"], "w_dn": sh["dn3"], "par": pars[c], "cst": C["cst"],
            "x1i": x1, "h2i": h2}


_NC_CACHE = {}


def _prog(part):
    if part not in _NC_CACHE:
        _NC_CACHE[part] = build_program(part)
    return _NC_CACHE[part]


FUSED = True


def _maps0(C, sh, pars, xhs, c):
    return {"xT": sh["xT"], "w_u": sh["w_u"], "w_fou": sh["w_fou"], "tab": C["tab"][c], "cc128": C["cc128"],
            "par": pars[c], "cst": C["cst"], "xh": xhs[c], "w_qkv": sh["w_qkv"], "w_out": sh["w_out"],
            "w_upg": sh["upg"], "w_upv": sh["upv"], "w_dn": sh["dn"], "abias": C["abias"]}


def kernel(**inputs):
    C, sh, pars, xhs = _prep(inputs)
    cores = list(range(NCORES))
    if FUSED:
        r = run_bass_kernel_spmd(_prog(0), [_maps0(C, sh, pars, xhs, c) for c in cores], core_ids=cores).results
        out = np.empty((1, SEQ, D), np.float32)
        for c in cores:
            out[0, c * TOK:(c + 1) * TOK, :] = np.asarray(r[c]["yT"], np.float32).T
        return out
    r1 = run_bass_kernel_spmd(_prog(1), [_maps1(C, sh, pars, xhs, c) for c in cores], core_ids=cores).results
    r2 = run_bass_kernel_spmd(_prog(2), [_maps2(C, sh, pars, xhs, c, np.asarray(r1[c]["fT"])) for c in cores],
                              core_ids=cores).results
    r3 = run_bass_kernel_spmd(_prog(3), [_maps3(C, sh, pars, xhs, c, np.asarray(r2[c]["x1o"]), np.asarray(r2[c]["h2o"]))
                                         for c in cores], core_ids=cores).results
    out = np.empty((1, SEQ, D), np.float32)
    for c in cores:
        out[0, c * TOK:(c + 1) * TOK, :] = np.asarray(r3[c]["yT"], np.float32).T
    return out
```
